# Optimizing a Trainium2 kernel written in Bass

```python
import jax, jax.numpy as jnp
from jax import lax
import numpy as np

D_MODEL = 1024
BATCH = 4
SEQ = 8192
DEPTH = 1

MLA_HEADS = 8
QK_NOPE_DIM = 128
QK_ROPE_DIM = 64
QK_HEAD_DIM = QK_NOPE_DIM + QK_ROPE_DIM
V_HEAD_DIM = 128
Q_LORA_RANK = 256
KV_LORA_RANK = 256
ROPE_THETA = 10000.0
Q_BLOCK = 128

LRU_WIDTH = D_MODEL
LRU_BLOCKS = 16
LRU_BLOCK_DIM = LRU_WIDTH // LRU_BLOCKS
CONV_WIDTH = 4
LRU_C = 8.0

FFN_HIDDEN = -(-8 * D_MODEL // (3 * 256)) * 256

EPS = 1e-6

COL_Q = Q_LORA_RANK
COL_KV = KV_LORA_RANK + QK_ROPE_DIM
COL_LRU_X = LRU_WIDTH
COL_LRU_G = LRU_WIDTH
COL_GATE_A = D_MODEL
COL_GATE_B = D_MODEL
IN_WIDTH = COL_Q + COL_KV + COL_LRU_X + COL_LRU_G + COL_GATE_A + COL_GATE_B

kernel_name = "hybrid_mla_rglru_gated_block"


def rms_norm(x, g):
    xf = x.astype(jnp.float32)
    xf = xf * lax.rsqrt(jnp.mean(xf * xf, axis=-1, keepdims=True) + EPS)
    return xf.astype(x.dtype) * g


def rope_tables(positions):
    half = QK_ROPE_DIM // 2
    inv_freq = ROPE_THETA ** (-jnp.arange(half, dtype=jnp.float32) / half)
    ang = positions.astype(jnp.float32)[..., None] * inv_freq
    return jnp.cos(ang)[:, :, None, :], jnp.sin(ang)[:, :, None, :]


def apply_rope(x, cos, sin):
    half = QK_ROPE_DIM // 2
    xf = x.astype(jnp.float32)
    x1, x2 = xf[..., :half], xf[..., half:]
    return jnp.concatenate([x1 * cos - x2 * sin, x2 * cos + x1 * sin], axis=-1).astype(x.dtype)


def head_qk_norm(t, g):
    return jnp.concatenate([rms_norm(t[..., :QK_NOPE_DIM], g[:QK_NOPE_DIM]),
                            rms_norm(t[..., QK_NOPE_DIM:], g[QK_NOPE_DIM:])], axis=-1)


def causal_attention_blocks(q, k, v):
    B, H, S, Dq = q.shape
    nblk = S // Q_BLOCK
    scale = QK_HEAD_DIM ** -0.5
    q_blocks = q.reshape(B, H, nblk, Q_BLOCK, Dq).transpose(2, 0, 1, 3, 4)
    key_idx = jnp.arange(S)

    def one_block(args):
        qb, bi = args
        s = jnp.einsum('bhqd,bhkd->bhqk', qb, k).astype(jnp.float32) * scale
        q_idx = bi * Q_BLOCK + jnp.arange(Q_BLOCK)
        mask = key_idx[None, :] <= q_idx[:, None]
        s = jnp.where(mask, s, -jnp.inf)
        p = jax.nn.softmax(s, axis=-1).astype(v.dtype)
        return jnp.einsum('bhqk,bhkd->bhqd', p, v)

    out = lax.map(one_block, (q_blocks, jnp.arange(nblk)))
    return out.transpose(1, 0, 3, 2, 4).reshape(B, S, H * V_HEAD_DIM)


def mla_branch(c_q_raw, ckv_raw, cos, sin, q_lat_g, w_q_up, kv_lat_g, w_kv_up, q_head_g, k_head_g):
    B, S, _ = c_q_raw.shape
    c_q = rms_norm(c_q_raw, q_lat_g)
    q = (c_q @ w_q_up).reshape(B, S, MLA_HEADS, QK_HEAD_DIM)
    c_kv = rms_norm(ckv_raw[..., :KV_LORA_RANK], kv_lat_g)
    k_rope = ckv_raw[..., KV_LORA_RANK:][:, :, None, :]
    kv = (c_kv @ w_kv_up).reshape(B, S, MLA_HEADS, QK_NOPE_DIM + V_HEAD_DIM)
    k_nope, v = kv[..., :QK_NOPE_DIM], kv[..., QK_NOPE_DIM:]
    q = head_qk_norm(q, q_head_g)
    k_nope = rms_norm(k_nope, k_head_g[:QK_NOPE_DIM])
    k_rope = rms_norm(k_rope, k_head_g[QK_NOPE_DIM:])
    q = jnp.concatenate([q[..., :QK_NOPE_DIM], apply_rope(q[..., QK_NOPE_DIM:], cos, sin)], axis=-1)
    k_rope = jnp.broadcast_to(apply_rope(k_rope, cos, sin), (B, S, MLA_HEADS, QK_ROPE_DIM))
    k = jnp.concatenate([k_nope, k_rope], axis=-1)
    return causal_attention_blocks(q.transpose(0, 2, 1, 3), k.transpose(0, 2, 1, 3), v.transpose(0, 2, 1, 3))


def rglru_branch(x_b, g_b, conv_w, conv_b, lru_wa, lru_ba, lru_wx, lru_bx, lru_lambda):
    B, S, _ = x_b.shape
    xp = jnp.pad(x_b, ((0, 0), (CONV_WIDTH - 1, 0), (0, 0)))
    xc = conv_b + xp[:, 0:S] * conv_w[0]
    for tap in range(1, CONV_WIDTH):
        xc = xc + xp[:, tap:tap + S] * conv_w[tap]
    xr = xc.reshape(B, S, LRU_BLOCKS, LRU_BLOCK_DIM)
    r = jax.nn.sigmoid(jnp.einsum('bsnd,nde->bsne', xr, lru_wa).reshape(B, S, LRU_WIDTH) + lru_ba)
    i = jax.nn.sigmoid(jnp.einsum('bsnd,nde->bsne', xr, lru_wx).reshape(B, S, LRU_WIDTH) + lru_bx)
    log_a = -LRU_C * r.astype(jnp.float32) * jax.nn.softplus(-lru_lambda.astype(jnp.float32))
    a = jnp.exp(log_a)
    mult = jnp.sqrt(-jnp.expm1(2.0 * log_a))
    b = mult * (i * xc).astype(jnp.float32)

    def combine(left, right):
        a1, b1 = left
        a2, b2 = right
        return a1 * a2, a2 * b1 + b2

    _, h = lax.associative_scan(combine, (a, b), axis=1)
    return h.astype(x_b.dtype) * jax.nn.gelu(g_b)


def setup_inputs(seed: int = 0) -> dict:
    key = jax.random.key(seed)
    ks = jax.random.split(key, 24)
    f32 = jnp.float32
    L = DEPTH

    def nrm(k, shape, fan_in):
        return jax.random.normal(k, shape, f32) * (fan_in ** -0.5)

    def gain(k, shape):
        return 1.0 + 0.02 * jax.random.normal(k, shape, f32)

    x = jax.random.normal(ks[0], (BATCH, SEQ, D_MODEL), f32)
    offsets = jax.random.randint(ks[1], (BATCH, 1), 0, 1024, dtype=jnp.int32)
    positions = offsets + jnp.arange(SEQ, dtype=jnp.int32)[None, :]
    a0 = jax.random.uniform(ks[17], (L, LRU_WIDTH), f32, 0.9, 0.999)
    return {
        "x": x,
        "positions": positions,
        "norm_mix_g": gain(ks[2], (L, D_MODEL)),
        "w_in": nrm(ks[3], (L, D_MODEL, IN_WIDTH), D_MODEL),
        "q_lat_g": gain(ks[4], (L, Q_LORA_RANK)),
        "w_q_up": nrm(ks[5], (L, Q_LORA_RANK, MLA_HEADS * QK_HEAD_DIM), Q_LORA_RANK),
        "kv_lat_g": gain(ks[6], (L, KV_LORA_RANK)),
        "w_kv_up": nrm(ks[7], (L, KV_LORA_RANK, MLA_HEADS * (QK_NOPE_DIM + V_HEAD_DIM)), KV_LORA_RANK),
        "q_head_g": gain(ks[8], (L, QK_HEAD_DIM)),
        "k_head_g": gain(ks[9], (L, QK_HEAD_DIM)),
        "conv_w": nrm(ks[10], (L, CONV_WIDTH, LRU_WIDTH), CONV_WIDTH),
        "conv_b": 0.02 * jax.random.normal(ks[11], (L, LRU_WIDTH), f32),
        "lru_wa": nrm(ks[12], (L, LRU_BLOCKS, LRU_BLOCK_DIM, LRU_BLOCK_DIM), LRU_BLOCK_DIM),
        "lru_ba": 0.02 * jax.random.normal(ks[13], (L, LRU_WIDTH), f32),
        "lru_wx": nrm(ks[14], (L, LRU_BLOCKS, LRU_BLOCK_DIM, LRU_BLOCK_DIM), LRU_BLOCK_DIM),
        "lru_bx": 0.02 * jax.random.normal(ks[15], (L, LRU_WIDTH), f32),
        "lru_lambda": jnp.log(a0) - jnp.log1p(-a0),
        "w_proj_attn": nrm(ks[16], (L, MLA_HEADS * V_HEAD_DIM, D_MODEL), MLA_HEADS * V_HEAD_DIM),
        "w_proj_lru": nrm(ks[18], (L, LRU_WIDTH, D_MODEL), LRU_WIDTH),
        "w_out": nrm(ks[19], (L, D_MODEL, D_MODEL), D_MODEL),
        "norm_ffn_g": gain(ks[20], (L, D_MODEL)),
        "w_ffn_gate": nrm(ks[21], (L, D_MODEL, FFN_HIDDEN), D_MODEL),
        "w_ffn_up": nrm(ks[22], (L, D_MODEL, FFN_HIDDEN), D_MODEL),
        "w_ffn_down": nrm(ks[23], (L, FFN_HIDDEN, D_MODEL), FFN_HIDDEN),
    }


def reference(x, positions, norm_mix_g, w_in, q_lat_g, w_q_up, kv_lat_g, w_kv_up, q_head_g, k_head_g,
              conv_w, conv_b, lru_wa, lru_ba, lru_wx, lru_bx, lru_lambda, w_proj_attn, w_proj_lru,
              w_out, norm_ffn_g, w_ffn_gate, w_ffn_up, w_ffn_down):
    split_points = [COL_Q, COL_Q + COL_KV, COL_Q + COL_KV + COL_LRU_X,
                    COL_Q + COL_KV + COL_LRU_X + COL_LRU_G,
                    COL_Q + COL_KV + COL_LRU_X + COL_LRU_G + COL_GATE_A]
    cos, sin = rope_tables(positions)
    for l in range(DEPTH):
        h = rms_norm(x, norm_mix_g[l])
        proj = h @ w_in[l]
        c_q_raw, ckv_raw, x_lru, g_lru, gate_a, gate_b = jnp.split(proj, split_points, axis=-1)
        y_a = mla_branch(c_q_raw, ckv_raw, cos, sin, q_lat_g[l], w_q_up[l], kv_lat_g[l], w_kv_up[l],
                         q_head_g[l], k_head_g[l])
        y_b = rglru_branch(x_lru, g_lru, conv_w[l], conv_b[l], lru_wa[l], lru_ba[l], lru_wx[l],
                           lru_bx[l], lru_lambda[l])
        merged = (jax.nn.sigmoid(gate_a) * (y_a @ w_proj_attn[l])
                  + jax.nn.sigmoid(gate_b) * (y_b @ w_proj_lru[l]))
        x = x + merged @ w_out[l]
        h2 = rms_norm(x, norm_ffn_g[l])
        x = x + (jax.nn.silu(h2 @ w_ffn_gate[l]) * (h2 @ w_ffn_up[l])) @ w_ffn_down[l]
    return x
```

```python
import contextlib
import numpy as np
import concourse.bass as bass
import concourse.mybir as mybir
from concourse.bass_utils import run_bass_kernel_spmd

F32 = mybir.dt.float32
BF16 = mybir.dt.bfloat16
I32 = mybir.dt.int32
AF = mybir.ActivationFunctionType
ALU = mybir.AluOpType

D = 1024
S = 8192
NH = 8
TT = 512
NT_ALL = S // TT
NT_OWN = NT_ALL // 2
S_OWN = S // 2
FF = 2816
NF = FF // 128
EPS = 1e-6
ATT_SCALE = 192 ** -0.5
ENGS = ("pe", "act", "dve", "pool", "sp")


class Res:
    __slots__ = ("name", "last_w", "readers")

    def __init__(self, name=""):
        self.name = name
        self.last_w = None
        self.readers = []


class Op:
    __slots__ = ("eng", "fn", "deps", "signaled", "sig", "is_dma", "key", "cum", "dmacnt")

    def __init__(self, eng, fn):
        self.eng = eng
        self.fn = fn
        self.deps = []
        self.signaled = False
        self.sig = 0
        self.is_dma = False
        self.key = None
        self.cum = 0
        self.dmacnt = 0


class _Rec:
    def __init__(self):
        self.calls = []

    def __getattr__(self, name):
        def f(*a, **k):
            self.calls.append((name, a, k))
            return None
        return f


def _record(fn):
    if fn is None:
        return None
    r = _Rec()
    fn(r)
    calls = r.calls

    def replay(eng):
        return [getattr(eng, n)(*a, **k) for (n, a, k) in calls]
    return replay


class Prog:
    def __init__(self, nc):
        self.nc = nc
        self.ops = []
        self.dma_cum = {}
        self.last_dma = {}
        self.final_waits = []

    def _track(self, op, reads, writes):
        deps = {}
        for r in reads:
            if r.last_w is not None:
                deps[id(r.last_w)] = r.last_w
        for w in writes:
            if w.last_w is not None:
                deps[id(w.last_w)] = w.last_w
            for rd in w.readers:
                deps[id(rd)] = rd
        for d in deps.values():
            if d.is_dma:
                op.deps.append((d, self.dma_cum[d.key]))
            else:
                if d.eng == op.eng and not op.is_dma and d.eng == "pe":
                    continue
                d.signaled = True
                op.deps.append((d, None))
        for r in reads:
            r.readers.append(op)
        for w in writes:
            w.last_w = op
            w.readers = []

    def op(self, eng, fn, reads=(), writes=()):
        o = Op(eng, _record(fn))
        self._track(o, reads, writes)
        self.ops.append(o)
        return o

    def dma(self, queue, key, fn, reads=(), writes=(), n=1):
        o = Op(queue, _record(fn))
        o.is_dma = True
        o.key = key
        self._track(o, reads, writes)
        self.dma_cum[key] = self.dma_cum.get(key, 0) + 16 * n
        o.cum = self.dma_cum[key]
        o.dmacnt = n
        self.last_dma[key] = o
        self.ops.append(o)
        return o

    def barrier(self):
        last = {}
        for o in self.ops:
            if not o.is_dma and o.fn is not None:
                last[o.eng] = o
        for e in ENGS:
            b = Op(e, None)
            for e2, o in last.items():
                if e2 != e:
                    o.signaled = True
                    b.deps.append((o, None))
            for k, d in self.last_dma.items():
                b.deps.append((d, self.dma_cum[k]))
            self.ops.append(b)

    def finish(self, dma_ops):
        self.final_waits = [(d, self.dma_cum[d.key]) for d in dma_ops]

    def emit(self):
        nc = self.nc
        cnt = {e: 0 for e in ENGS}
        for o in self.ops:
            if o.is_dma:
                continue
            if o.signaled:
                cnt[o.eng] += 1
                o.sig = cnt[o.eng]
        keys = sorted(self.dma_cum.keys())
        with contextlib.ExitStack() as st:
            esem = {e: st.enter_context(nc.semaphore("S_" + e)) for e in ("pe", "act", "dve", "pool")}
            dsem = {k: st.enter_context(nc.semaphore("D_" + k)) for k in keys}
            block = st.enter_context(nc.Block())
            ops = self.ops
            final_waits = self.final_waits

            def run(engname, eng):
                waited = {}
                for o in ops:
                    if o.eng != engname:
                        continue
                    for (d, cum) in o.deps:
                        if d.is_dma:
                            s, v, kk = dsem[d.key], cum, ("d", d.key)
                        else:
                            s, v, kk = esem[d.eng], d.sig, ("e", d.eng)
                        if waited.get(kk, 0) >= v:
                            continue
                        waited[kk] = v
                        eng.wait_ge(s, v)
                    if o.fn is None:
                        continue
                    r = o.fn(eng)
                    if o.is_dma:
                        assert len(r) == o.dmacnt, (len(r), o.dmacnt, o.key)
                        for ins in r:
                            ins.then_inc(dsem[o.key], 16)
                    elif o.signaled:
                        r[-1].then_inc(esem[o.eng], 1)
                if engname == "sp":
                    for (d, cum) in final_waits:
                        eng.wait_ge(dsem[d.key], cum)

            @block.sync
            def _(e):
                run("sp", e)

            @block.tensor
            def _(e):
                run("pe", e)

            @block.scalar
            def _(e):
                run("act", e)

            @block.vector
            def _(e):
                run("dve", e)

            @block.gpsimd
            def _(e):
                run("pool", e)
        return cnt


class Ring:
    def __init__(self, st, nc, name, n, shape, dt, psum=False):
        self.items = []
        for i in range(n):
            if psum:
                t = st.enter_context(nc.psum_tensor("%s%d" % (name, i), shape, dt))
            else:
                t = st.enter_context(nc.sbuf_tensor("%s%d" % (name, i), shape, dt))
            self.items.append((t, Res("%s%d" % (name, i)), i))
        self.i = 0

    def next(self):
        it = self.items[self.i % len(self.items)]
        self.i += 1
        return it


V_GMIX, V_GFFN, V_GQL, V_GKVL = 0, 8, 16, 18
V_GQN, V_GQR, V_GQS, V_GKN, V_GKR, V_GKS, V_SGN, V_INVF = 20, 21, 22, 23, 24, 25, 26, 27
V_CONVW, V_CONVB, V_BA, V_BX, V_LAM, V_M0, V_M1 = 28, 60, 68, 76, 84, 92, 93
NV = 94
DV_CLAM, DV_CLAM2, DV_GQSS, DV_GKSS, DV_NHALF, DV_ONE, DV_EPS = 0, 8, 16, 17, 18, 22, 23
NDV = 24

C1W = 6.28125
C2W = float(2 * np.pi - 6.28125)
PI = float(np.pi)
TWO_PI = float(2 * np.pi)


def build_program(debug=False):
    PH = {p.split(":")[0]: (int(p.split(":")[1]) if ":" in p else None) for p in _PHASES}

    def lim(name, n):
        if name not in PH:
            return 0
        return n if PH[name] is None else min(n, PH[name])
    nc = bass.Bass("TRN2", target_bir_lowering=False)

    def din(name, shape, dt=F32):
        return nc.dram_tensor(name, list(shape), dt, kind="ExternalInput").ap()

    x_all = din("x_all", [S, D])
    x_own = din("x_own", [S_OWN, D])
    pos_all = din("pos_all", [1, S], I32)
    pos_own = din("pos_own", [1, S_OWN], I32)
    winA1 = din("winA1", [D, 1536])
    winA2 = din("winA2", [D, 1280])
    winC = din("winC", [D, 2048])
    wkvK = din("wkvK", [256, 1024])
    wkvV = din("wkvV", [256, 1024])
    wqN = din("wqN", [256, 1024])
    wqR = din("wqR", [256, 512])
    wqS = din("wqS", [256, 512])
    lruA = din("lruA", [8, 128, 128])
    lruX = din("lruX", [8, 128, 128])
    vecs = din("vecs", [128, NV])
    masks = din("masks", [8, 128, TT])
    wpa = din("wpa", [D, D])
    wpl = din("wpl", [D, D])
    wo = din("wo", [D, D])
    wfg = din("wfg", [D, FF])
    wfu = din("wfu", [D, FF])
    wfd = din("wfd", [FF, D])
    out = nc.dram_tensor("out", [S_OWN, D], F32, kind="ExternalOutput").ap()

    def scratch(name, shape, dt):
        return nc.dram_tensor(name, list(shape), dt).ap()

    KT = scratch("KT", [NH, 128, S], BF16)
    KR = scratch("KR", [128, S], BF16)
    VS = scratch("VS", [NH, 128, S], BF16)
    QN = scratch("QN", [NH, 128, S_OWN], BF16)
    QR = scratch("QR", [NH // 2, 128, S_OWN], BF16)
    HS = scratch("HS", [NT_ALL, 8, 128, TT], F32)
    YB = scratch("YB", [8, 128, S_OWN], BF16)
    YA = scratch("YA", [NH, 128, S_OWN], BF16)
    X1 = scratch("X1", [S_OWN, D], F32)
    R_KT, R_KR, R_VS, R_QN, R_QR, R_HS, R_YB, R_YA, R_X1 = [Res(n) for n in
                                                           ("KT", "KR", "VS", "QN", "QR", "HS", "YB", "YA", "X1")]

    P = Prog(nc)
    out_dmas = []

    with contextlib.ExitStack() as gst:
        def gt(name, shape, dt):
            return gst.enter_context(nc.sbuf_tensor(name, shape, dt))

        vec = gt("vec", [128, NV], F32)
        dvec = gt("dvec", [128, NDV], F32)
        ident = gt("ident", [128, 128], BF16)
        ones = gt("ones", [128, 128], BF16)
        R_vec, R_dvec, R_ident, R_ones = Res("vec"), Res("dvec"), Res("ident"), Res("ones")

        def vcol(c, n=1):
            return vec[:, c:c + n]

        def dcol(c, n=1):
            return dvec[:, c:c + n]

        P.dma("sp", "const", lambda e: e.dma_start(out=vec[:], in_=vecs), writes=[R_vec])
        P.op("pool", lambda e: e.memset(ident[:], 1.0), writes=[R_ident])
        P.op("pool", lambda e: e.affine_select(out=ident[:], in_=ident[:], pattern=[[-1, 128]],
                                               compare_op=ALU.is_equal, fill=0.0, base=0,
                                               channel_multiplier=1), reads=[R_ident], writes=[R_ident])
        P.op("pool", lambda e: e.memset(ones[:], 1.0), writes=[R_ones])
        P.op("act", lambda e: e.activation(out=dcol(DV_CLAM, 8), in_=vcol(V_LAM, 8), func=AF.Exp, scale=-1.0),
             reads=[R_vec], writes=[R_dvec])
        P.op("act", lambda e: e.activation(out=dcol(DV_CLAM, 8), in_=dcol(DV_CLAM, 8), func=AF.Ln, bias=1.0),
             reads=[R_dvec], writes=[R_dvec])
        P.op("dve", lambda e: e.tensor_scalar(out=dcol(DV_CLAM2, 8), in0=dcol(DV_CLAM, 8), scalar1=-16.0,
                                              scalar2=None, op0=ALU.mult), reads=[R_dvec], writes=[R_dvec])
        P.op("dve", lambda e: e.tensor_scalar(out=dcol(DV_CLAM, 8), in0=dcol(DV_CLAM, 8), scalar1=-8.0,
                                              scalar2=None, op0=ALU.mult), reads=[R_dvec], writes=[R_dvec])
        P.op("dve", lambda e: e.tensor_tensor(out=dcol(DV_GQSS), in0=vcol(V_GQS), in1=vcol(V_SGN), op=ALU.mult),
             reads=[R_vec, R_dvec], writes=[R_dvec])
        P.op("dve", lambda e: e.tensor_tensor(out=dcol(DV_GKSS), in0=vcol(V_GKS), in1=vcol(V_SGN), op=ALU.mult),
             reads=[R_vec, R_dvec], writes=[R_dvec])
        P.op("pool", lambda e: e.memset(dcol(DV_NHALF, 4), -0.5), reads=[R_dvec], writes=[R_dvec])
        P.op("pool", lambda e: e.memset(dcol(DV_ONE), 1.0), reads=[R_dvec], writes=[R_dvec])
        P.op("pool", lambda e: e.memset(dcol(DV_EPS), EPS), reads=[R_dvec], writes=[R_dvec])

        def load_scaled_weight(st, name, src, nk, ncols, gcol, stage_ring, qkey, w=None):
            if w is None:
                w = st.enter_context(nc.sbuf_tensor(name, [128, nk, ncols], BF16))
            rw = Res(name)
            CH = 1024
            for kc in range(nk):
                for c0 in range(0, ncols, CH):
                    cw = min(CH, ncols - c0)
                    stg, rs, si = stage_ring.next()
                    P.dma("sp", "%s%d" % (qkey, si),
                          lambda e, stg=stg, kc=kc, c0=c0, cw=cw: e.dma_start(
                              out=stg[:, 0:cw], in_=src[kc * 128:(kc + 1) * 128, c0:c0 + cw]),
                          writes=[rs])
                    P.op("act", lambda e, stg=stg, kc=kc, c0=c0, cw=cw: e.activation(
                        out=w[:, kc, c0:c0 + cw], in_=stg[:, 0:cw], func=AF.Identity, scale=vcol(gcol + kc)),
                        reads=[rs, R_vec], writes=[rw])
            return w, rw

        def load_cast_weight(st, name, src, nk, ncols, key):
            w = st.enter_context(nc.sbuf_tensor(name, [128, nk, ncols], BF16))
            rw = Res(name)
            for kc in range(nk):
                P.dma("pool", key, lambda e, kc=kc: e.dma_start(out=w[:, kc, :], in_=src[kc * 128:(kc + 1) * 128, :]),
                      writes=[rw])
            return w, rw

        def rms_to_hT(xt, rx, ss, r_ss, hb_ring, tp_ring, hT, r_hT_parts, s):
            hb, rhb, _ = hb_ring.next()
            sc = ss[:, s:s + 1]
            rsc = r_ss[s]
            P.op("act", lambda e: e.activation(out=hb[:], in_=xt[:], func=AF.Square, accum_out=sc),
                 reads=[rx], writes=[rhb, rsc])
            P.op("dve", lambda e: e.tensor_scalar(out=sc, in0=sc, scalar1=1.0 / D,
                                                  scalar2=EPS, op0=ALU.mult, op1=ALU.add), reads=[rsc], writes=[rsc])
            P.op("pool", lambda e: e.tensor_tensor(out=sc, in0=sc, in1=dcol(DV_NHALF),
                                                   op=ALU.pow), reads=[rsc, R_dvec], writes=[rsc])
            P.op("act", lambda e: e.activation(out=hb[:], in_=xt[:], func=AF.Identity, scale=sc),
                 reads=[rx, rsc], writes=[rhb])
            tp, rtp, _ = tp_ring.next()

            def tr(e):
                for kc in range(8):
                    e.transpose(out=tp[:, kc * 128:(kc + 1) * 128], in_=hb[:, kc * 128:(kc + 1) * 128],
                                identity=ident[:])
            P.op("pe", tr, reads=[rhb, R_ident], writes=[rtp])
            P.op("dve", lambda e: e.tensor_copy(out=hT[:, :, s * 128:(s + 1) * 128],
                                                in_=tp[:].rearrange("p (k t) -> p k t", k=8)),
                 reads=[rtp], writes=[r_hT_parts[s]])

        def mm(e, out_ap, pairs):
            r = None
            n = len(pairs)
            for i, (l, rr) in enumerate(pairs):
                r = e.matmul(out_ap, lhsT=l, rhs=rr, start=(i == 0), stop=(i == n - 1))
            return r

        def rope_tables(st_ring, pos_src, t0, ctab, stab, r_tab):
            ti, r_ti, _ = st_ring["i"].next()
            ta, r_ta, _ = st_ring["a"].next()
            tk, r_tk, _ = st_ring["k"].next()
            P.dma("sp", "pos", lambda e: e.dma_start(out=ti[:], in_=pos_src[:, t0:t0 + TT].partition_broadcast(128)),
                  writes=[r_ti])
            P.op("dve", lambda e: e.tensor_copy(out=ta[:], in_=ti[:]), reads=[r_ti], writes=[r_ta])
            P.op("dve", lambda e: e.tensor_scalar(out=ta[:], in0=ta[:], scalar1=vcol(V_INVF), scalar2=None, op0=ALU.mult),
                 reads=[r_ta, R_vec], writes=[r_ta])
            P.op("dve", lambda e: e.tensor_scalar(out=ti[:], in0=ta[:], scalar1=1.0 / TWO_PI, scalar2=None, op0=ALU.mult),
                 reads=[r_ta], writes=[r_ti])
            P.op("dve", lambda e: e.tensor_copy(out=tk[:], in_=ti[:]), reads=[r_ti], writes=[r_tk])
            P.op("dve", lambda e: e.scalar_tensor_tensor(out=ta[:], in0=tk[:], scalar=-C1W, in1=ta[:], op0=ALU.mult,
                                                         op1=ALU.add), reads=[r_tk, r_ta], writes=[r_ta])
            P.op("dve", lambda e: e.scalar_tensor_tensor(out=ta[:], in0=tk[:], scalar=-C2W, in1=ta[:], op0=ALU.mult,
                                                         op1=ALU.add), reads=[r_tk, r_ta], writes=[r_ta])

            def fold(buf, rbuf):
                P.op("dve", lambda e: e.tensor_scalar(out=tk[:], in0=buf[:], scalar1=PI, scalar2=-TWO_PI, op0=ALU.is_gt,
                                                      op1=ALU.mult), reads=[rbuf], writes=[r_tk])
                P.op("dve", lambda e: e.tensor_tensor(out=buf[:], in0=buf[:], in1=tk[:], op=ALU.add),
                     reads=[rbuf, r_tk], writes=[rbuf])
                P.op("dve", lambda e: e.tensor_scalar(out=tk[:], in0=buf[:], scalar1=-PI, scalar2=TWO_PI, op0=ALU.is_lt,
                                                      op1=ALU.mult), reads=[rbuf], writes=[r_tk])
                P.op("dve", lambda e: e.tensor_tensor(out=buf[:], in0=buf[:], in1=tk[:], op=ALU.add),
                     reads=[rbuf, r_tk], writes=[rbuf])
            fold(ta, r_ta)
            P.op("act", lambda e: e.activation(out=stab[:], in_=ta[:], func=AF.Sin), reads=[r_ta], writes=[r_tab])
            P.op("dve", lambda e: e.tensor_scalar(out=ta[:], in0=ta[:], scalar1=PI / 2, scalar2=None, op0=ALU.add),
                 reads=[r_ta, r_tab], writes=[r_ta])
            fold(ta, r_ta)
            P.op("act", lambda e: e.activation(out=ctab[:], in_=ta[:], func=AF.Sin), reads=[r_ta], writes=[r_tab])

        def rstd_bcast(sum_ps, r_sum, scale, add_ap_or_const, r_add, outt, r_out):
            if isinstance(add_ap_or_const, float):
                P.op("dve", lambda e: e.tensor_scalar(out=outt[:], in0=sum_ps[:], scalar1=scale, scalar2=add_ap_or_const,
                                                      op0=ALU.mult, op1=ALU.add), reads=[r_sum], writes=[r_out])
            else:
                P.op("dve", lambda e: e.scalar_tensor_tensor(out=outt[:], in0=sum_ps[:], scalar=scale,
                                                             in1=add_ap_or_const, op0=ALU.mult, op1=ALU.add),
                     reads=[r_sum, r_add], writes=[r_out])
            P.op("act", lambda e: e.activation(out=outt[:], in_=outt[:], func=AF.Sqrt), reads=[r_out], writes=[r_out])
            P.op("dve", lambda e: e.reciprocal(out=outt[:], in_=outt[:]), reads=[r_out], writes=[r_out])

        def run_rr(gens):
            gens = list(gens)
            while gens:
                for g in list(gens):
                    try:
                        next(g)
                    except StopIteration:
                        gens.remove(g)

        def drain(g):
            for _ in g:
                pass

        def rms_tile(x_src, row0, xring, xkey, ss, r_ss, hbr, tpr, hT, r_hT, xs_out=None, rd=()):
            its = []
            for s in range(4):
                xt, rx, si = xring.next()
                if xs_out is not None:
                    xs_out.append((xt, rx, si))
                r0 = row0 + s * 128
                P.dma("sp", "%s%d" % (xkey, si), lambda e: e.dma_start(out=xt[:], in_=x_src[r0:r0 + 128, :]), reads=list(rd), writes=[rx])
                hb, rhb, _ = hbr.next()
                its.append((xt, rx, hb, rhb, ss[:, s:s + 1], r_ss[s]))
            yield
            for (xt, rx, hb, rhb, sc, rsc) in its:
                P.op("act", lambda e: e.activation(out=hb[:], in_=xt[:], func=AF.Square, accum_out=sc), reads=[rx], writes=[rhb, rsc])
            yield
            for (xt, rx, hb, rhb, sc, rsc) in its:
                P.op("dve", lambda e: e.tensor_scalar(out=sc, in0=sc, scalar1=1.0 / D, scalar2=EPS, op0=ALU.mult, op1=ALU.add),
                     reads=[rsc], writes=[rsc])
            yield
            for (xt, rx, hb, rhb, sc, rsc) in its:
                P.op("pool", lambda e: e.tensor_tensor(out=sc, in0=sc, in1=dcol(DV_NHALF), op=ALU.pow), reads=[rsc, R_dvec], writes=[rsc])
            yield
            for (xt, rx, hb, rhb, sc, rsc) in its:
                P.op("act", lambda e: e.activation(out=hb[:], in_=xt[:], func=AF.Identity, scale=sc), reads=[rx, rsc], writes=[rhb])
            yield
            for s, (xt, rx, hb, rhb, sc, rsc) in enumerate(its):
                tp, rtp, _ = tpr.next()

                def tr(e):
                    for kc in range(8):
                        e.transpose(out=tp[:, kc * 128:(kc + 1) * 128], in_=hb[:, kc * 128:(kc + 1) * 128], identity=ident[:])
                P.op("pe", tr, reads=[rhb, R_ident], writes=[rtp])
                P.op("dve", lambda e: e.tensor_copy(out=hT[:, :, s * 128:(s + 1) * 128], in_=tp[:].rearrange("p (k t) -> p k t", k=8)),
                     reads=[rtp], writes=[r_hT[s]])


            yield

        def norm_chain(items, psr, ones_l, r_ones_l, inv_n, add_b, r_add_b, add_const, do_square=True):
            if do_square:
                for it in items:
                    P.op("act", lambda e: e.activation(out=it["sq"][:], in_=it["ps"][:], func=AF.Square), reads=[it["r_ps"]], writes=[it["r_sq"]])
                yield
            for it in items:
                sps, r_sps, _ = psr.next()
                pairs = [(ones_l[:], it["sq"][:])]
                rd = [r_ones_l, it["r_sq"]]
                if add_b is not None:
                    pairs.append((ident[:], add_b[:]))
                    rd += [R_ident, r_add_b]
                P.op("pe", lambda e: mm(e, sps[:], pairs), reads=rd, writes=[r_sps])
                if add_b is not None:
                    P.op("act", lambda e: e.activation(out=it["rs"][:], in_=sps[:], func=AF.Ln, scale=inv_n),
                         reads=[r_sps], writes=[it["r_rs"]])
                else:
                    P.op("act", lambda e: e.activation(out=it["rs"][:], in_=sps[:], func=AF.Ln, scale=inv_n, bias=dcol(DV_EPS)),
                         reads=[r_sps, R_dvec], writes=[it["r_rs"]])
            yield
            for it in items:
                P.op("act", lambda e: e.activation(out=it["rs"][:], in_=it["rs"][:], func=AF.Exp, scale=-0.5),
                     reads=[it["r_rs"]], writes=[it["r_rs"]])
            yield

        with contextlib.ExitStack() as st:
            def T(name, shape, dt):
                return st.enter_context(nc.sbuf_tensor(name, shape, dt))
            wA1 = T("wA1", [128, 8, 1536], BF16)
            wK = T("wK", [128, 2, 1024], BF16)
            wV = T("wV", [128, 2, 1024], BF16)
            wLA, r_wLA = load_cast_weight(st, "wLA", lruA.rearrange("c p m -> (c p) m"), 8, 128, "wcast")
            wLX, r_wLX = load_cast_weight(st, "wLX", lruX.rearrange("c p m -> (c p) m"), 8, 128, "wcast")
            with contextlib.ExitStack() as st2:
                stage = Ring(st2, nc, "stgA1_", 2, [128, 1024], F32)
                _, r_wA1 = load_scaled_weight(st, "wA1", winA1, 8, 1536, V_GMIX, stage, "stg", w=wA1)
                _, r_wK = load_scaled_weight(st, "wK", wkvK, 2, 1024, V_GKVL, stage, "stg", w=wK)
                _, r_wV = load_scaled_weight(st, "wV", wkvV, 2, 1024, V_GKVL, stage, "stg", w=wV)
            P.barrier()

            xring = Ring(st, nc, "xA1_", 4, [128, D], F32)
            ssr = [(T("ssA1_%d" % i, [128, 4], F32), [Res() for _ in range(4)]) for i in range(2)]
            hbr = Ring(st, nc, "hbA1_", 4, [128, D], BF16)
            hTr = [(T("hTA1_%d" % i, [128, 8, TT], BF16), [Res() for _ in range(4)]) for i in range(2)]
            tpr = Ring(st, nc, "tpA1_", 1, [128, D], BF16, psum=True)
            psr = Ring(st, nc, "psA1_", 7, [128, TT], F32, psum=True)
            ckvT = T("ckvT", [128, 2, TT], BF16)
            r_ckvT = Res()
            sql = [(T("sqlA1_%d" % i, [128, TT], BF16), Res()) for i in range(2)]
            ekvb = T("ekvb", [128, TT], BF16)
            r_ekvb = Res()
            rvs = T("rvs", [128, 4], F32)
            r_rvs = Res()
            W4 = 8
            WK = 3
            kraw = [(T("krawA1_%d" % i, [128, TT], F32), Res()) for i in range(WK)]
            ksq = [(T("ksqA1_%d" % i, [128, TT], BF16), Res()) for i in range(WK + 1)]
            krs = [(T("krsA1_%d" % i, [128, TT], F32), Res()) for i in range(WK + 1)]
            kout = [(T("koutA1_%d" % i, [128, TT], BF16), Res()) for i in range(WK + 1)]
            vsb = Ring(st, nc, "vsbA1_", 2, [128, D], BF16)
            tabr = {"i": Ring(st, nc, "tbi_", 1, [128, TT], I32), "a": Ring(st, nc, "tba_", 1, [128, TT], F32),
                    "k": Ring(st, nc, "tbk_", 1, [128, TT], F32)}
            ctab = T("ctabA1", [128, TT], F32)
            stab = T("stabA1", [128, TT], F32)
            r_tab = Res()
            (t1, r_t1), (t2, r_t2) = kraw[0], kraw[1]
            XL = T("XL", [128, 8, TT + 3], F32)
            r_XL = [Res() for _ in range(8)]
            xcW = T("xcW", [128, W4, TT], F32)
            r_xc = [Res() for _ in range(W4)]
            lxcb = [(T("lxcb%d" % i, [128, TT], BF16), Res()) for i in range(W4)]
            lr_ = [(T("llr%d" % i, [128, TT], F32), Res()) for i in range(W4)]
            li_ = [(T("lli%d" % i, [128, TT], F32), Res()) for i in range(W4)]
            la_ = [(T("lla%d" % i, [128, TT], F32), Res()) for i in range(W4)]
            hlast = T("hlast", [128, 8], F32)
            r_hlast = [Res() for _ in range(8)]

            P.op("pool", lambda e: e.memset(XL[:], 0.0), writes=r_XL)
            P.op("pool", lambda e: e.memset(hlast[:], 0.0), writes=r_hlast)

            nA1 = lim("A1", NT_ALL)
            if nA1 > 0:
                drain(rms_tile(x_all, 0, xring, "xA1_", ssr[0][0], ssr[0][1], hbr, tpr, hTr[0][0], hTr[0][1]))
            for t in range(nA1):
                hT, r_hT = hTr[t % 2]
                rope_tables(tabr, pos_all, t * TT, ctab, stab, r_tab)

                def proj(c0):
                    ps, rps, _ = psr.next()
                    P.op("pe", lambda e: mm(e, ps[:], [(wA1[:, kc, c0:c0 + 128], hT[:, kc, :]) for kc in range(8)]),
                         reads=[r_wA1] + r_hT, writes=[rps])
                    return ps, rps

                lat = [proj(kc * 128) for kc in range(2)]
                for kc, (ps, rps) in enumerate(lat):
                    P.op("act", lambda e: e.activation(out=ckvT[:, kc, :], in_=ps[:], func=AF.Copy), reads=[rps], writes=[r_ckvT])
                    sq, rsq = sql[kc]
                    P.op("act", lambda e: e.activation(out=sq[:], in_=ps[:], func=AF.Square), reads=[rps], writes=[rsq])
                sum_ps, r_sum, _ = psr.next()
                P.op("pe", lambda e: mm(e, sum_ps[:], [(ones[:], sq[:]) for sq, _ in sql]), reads=[R_ones] + [r for _, r in sql], writes=[r_sum])
                P.op("dve", lambda e: e.tensor_scalar(out=ekvb[:], in0=sum_ps[:], scalar1=EPS / 2.0, scalar2=128.0 * EPS * EPS,
                                                      op0=ALU.mult, op1=ALU.add), reads=[r_sum], writes=[r_ekvb])
                sv_ps, r_sv, _ = psr.next()

                def svmm(e):
                    for s in range(4):
                        mm(e, sv_ps[:, s:s + 1], [(sq[:, s * 128:(s + 1) * 128], ones[:, 0:1]) for sq, _ in sql])
                P.op("pe", svmm, reads=[R_ones] + [r for _, r in sql], writes=[r_sv])
                P.op("dve", lambda e: e.tensor_scalar(out=rvs[:], in0=sv_ps[:, 0:4], scalar1=1.0 / 256.0, scalar2=EPS, op0=ALU.mult, op1=ALU.add),
                     reads=[r_sv], writes=[r_rvs])
                P.op("pool", lambda e: e.tensor_tensor(out=rvs[:], in0=rvs[:], in1=dcol(DV_NHALF, 4), op=ALU.pow),
                     reads=[r_rvs, R_dvec], writes=[r_rvs])

                def lru_window(c0):
                    items = []
                    for w in range(W4):
                        c = c0 + w
                        xps, r_xps = proj(512 + c * 128)
                        items.append(dict(c=c, w=w))
                        P.op("act", lambda e: e.activation(out=XL[:, c, 3:3 + TT], in_=xps[:], func=AF.Copy),
                             reads=[r_xps], writes=[r_XL[c]])
                        if w % 2 == 1:
                            yield
                    for p0 in range(0, W4, 2):
                        pair = items[p0:p0 + 2]
                        for tap in range(4):
                            for it in pair:
                                c, w = it["c"], it["w"]
                                cw = V_CONVW + c * 4
                                if tap == 0:
                                    P.op("dve", lambda e: e.tensor_scalar(out=xcW[:, w, :], in0=XL[:, c, 0:TT], scalar1=vcol(cw),
                                                                          scalar2=vcol(V_CONVB + c), op0=ALU.mult, op1=ALU.add),
                                         reads=[r_XL[c], R_vec], writes=[r_xc[w]])
                                else:
                                    P.op("dve", lambda e: e.scalar_tensor_tensor(out=xcW[:, w, :], in0=XL[:, c, tap:tap + TT], scalar=vcol(cw + tap),
                                                                                 in1=xcW[:, w, :], op0=ALU.mult, op1=ALU.add),
                                         reads=[r_XL[c], r_xc[w], R_vec], writes=[r_xc[w]])
                        for it in pair:
                            c, w = it["c"], it["w"]
                            P.op("pool", lambda e: e.tensor_copy(out=XL[:, c, 0:3], in_=XL[:, c, TT:TT + 3]), reads=[r_XL[c]], writes=[r_XL[c]])
                            xcb, r_xcb = lxcb[w]
                            P.op("dve", lambda e: e.tensor_copy(out=xcb[:], in_=xcW[:, w, :]), reads=[r_xc[w]], writes=[r_xcb])
                        yield
                        for it in pair:
                            c, w = it["c"], it["w"]
                            xcb, r_xcb = lxcb[w]
                            rps_, r_rps, _ = psr.next()
                            P.op("pe", lambda e: mm(e, rps_[:], [(wLA[:, c, :], xcb[:])]), reads=[r_wLA, r_xcb], writes=[r_rps])
                            lr, r_lr = lr_[w]
                            P.op("act", lambda e: e.activation(out=lr[:], in_=rps_[:], func=AF.Sigmoid, bias=vcol(V_BA + c)),
                                 reads=[r_rps, R_vec], writes=[r_lr])
                            ips_, r_ips, _ = psr.next()
                            P.op("pe", lambda e: mm(e, ips_[:], [(wLX[:, c, :], xcb[:])]), reads=[r_wLX, r_xcb], writes=[r_ips])
                            li, r_li = li_[w]
                            P.op("act", lambda e: e.activation(out=li[:], in_=ips_[:], func=AF.Sigmoid, bias=vcol(V_BX + c)),
                                 reads=[r_ips, R_vec], writes=[r_li])
                        yield
                    for it in items:
                        c, w = it["c"], it["w"]
                        lr, r_lr = lr_[w]
                        la, r_la = la_[w]
                        P.op("act", lambda e: e.activation(out=la[:], in_=lr[:], func=AF.Exp, scale=dcol(DV_CLAM + c)),
                             reads=[r_lr, R_dvec], writes=[r_la])
                    yield
                    for it in items:
                        c, w = it["c"], it["w"]
                        lr, r_lr = lr_[w]
                        P.op("act", lambda e: e.activation(out=lr[:], in_=lr[:], func=AF.Exp, scale=dcol(DV_CLAM2 + c)),
                             reads=[r_lr, R_dvec], writes=[r_lr])
                    yield
                    for it in items:
                        w = it["w"]
                        li, r_li = li_[w]
                        P.op("pool", lambda e: e.tensor_tensor(out=li[:], in0=li[:], in1=xcW[:, w, :], op=ALU.mult),
                             reads=[r_li, r_xc[w]], writes=[r_li])
                    yield
                    for it in items:
                        w = it["w"]
                        lr, r_lr = lr_[w]
                        P.op("act", lambda e: e.activation(out=lr[:], in_=lr[:], func=AF.Sqrt, scale=-1.0, bias=dcol(DV_ONE)),
                             reads=[r_lr, R_dvec], writes=[r_lr])
                    yield
                    for it in items:
                        w = it["w"]
                        li, r_li = li_[w]
                        lr, r_lr = lr_[w]
                        P.op("pool", lambda e: e.tensor_tensor(out=li[:], in0=li[:], in1=lr[:], op=ALU.mult), reads=[r_li, r_lr], writes=[r_li])
                    yield
                    for it in items:
                        c, w = it["c"], it["w"]
                        li, r_li = li_[w]
                        la, r_la = la_[w]
                        P.op("dve", lambda e: e.tensor_tensor_scan(out=xcW[:, w, :], data0=la[:], data1=li[:], initial=hlast[:, c:c + 1],
                                                                   op0=ALU.mult, op1=ALU.add),
                             reads=[r_la, r_li, r_hlast[c]], writes=[r_xc[w]])
                        P.op("act", lambda e: e.activation(out=hlast[:, c:c + 1], in_=xcW[:, w, TT - 1:TT], func=AF.Copy),
                             reads=[r_xc[w]], writes=[r_hlast[c]])
                    yield
                    P.dma("pool", "hst", lambda e: e.dma_start(out=HS[t, c0:c0 + W4].rearrange("c p t -> p c t"), in_=xcW[:]),
                          reads=r_xc, writes=[R_HS])


                def v_gen():
                    for s in range(4):
                        vt, rvt, vi = vsb.next()
                        for half in range(2):
                            ps, rps, _ = psr.next()
                            P.op("pe", lambda e: mm(e, ps[:], [(ckvT[:, kc, s * 128:(s + 1) * 128], wV[:, kc, half * 512:(half + 1) * 512])
                                                               for kc in range(2)]), reads=[r_ckvT, r_wV], writes=[rps])
                            P.op("act", lambda e: e.activation(out=vt[:, half * 512:(half + 1) * 512], in_=ps[:], func=AF.Identity,
                                                               scale=rvs[:, s:s + 1]), reads=[rps, r_rvs], writes=[rvt])
                        blk = t * 4 + s
                        P.dma("pool", "vst%d" % vi, lambda e: e.dma_start(
                            out=VS[:, :, blk * 128:(blk + 1) * 128].rearrange("h p d -> p h d"),
                            in_=vt[:].rearrange("p (h d) -> p h d", h=NH)), reads=[rvt], writes=[R_VS])
                    yield


                def k_window(h0, nh):
                    items = []
                    for w in range(nh):
                        h = h0 + w
                        kps, r_kps, _ = psr.next()
                        P.op("pe", lambda e: mm(e, kps[:], [(wK[:, kc, h * 128:(h + 1) * 128], ckvT[:, kc, :]) for kc in range(2)]),
                             reads=[r_wK, r_ckvT], writes=[r_kps])
                        it = dict(h=h, w=w, sq=ksq[w][0], r_sq=ksq[w][1], rs=krs[w][0], r_rs=krs[w][1])
                        items.append(it)
                        kr_, r_kr = kraw[w]
                        P.op("dve", lambda e: e.tensor_copy(out=kr_[:], in_=kps[:]), reads=[r_kps], writes=[r_kr])
                        P.op("act", lambda e: e.activation(out=it["sq"][:], in_=kps[:], func=AF.Square), reads=[r_kps], writes=[it["r_sq"]])
                    yield
                    yield from norm_chain(items, psr, ones, R_ones, 1.0 / 128.0, ekvb, r_ekvb, None, do_square=False)
                    for it in items:
                        w, h = it["w"], it["h"]
                        kr_, r_kr = kraw[w]
                        ko, r_ko = kout[w]
                        P.op("dve", lambda e: e.scalar_tensor_tensor(out=ko[:], in0=kr_[:], scalar=vcol(V_GKN), in1=it["rs"][:],
                                                                     op0=ALU.mult, op1=ALU.mult),
                             reads=[r_kr, it["r_rs"], R_vec], writes=[r_ko])
                        P.dma("pool", "kst%d" % w, lambda e: e.dma_start(out=KT[h, :, t * TT:(t + 1) * TT], in_=ko[:]),
                              reads=[r_ko], writes=[R_KT])
                    yield

                def k_gen():
                    yield from k_window(0, 3)
                    yield from k_window(3, 3)
                    yield from k_window(6, 2)
                    aps, r_aps = proj(256)
                    bps, r_bps = proj(384)
                    ritem = dict(ps=aps, r_ps=r_aps, sq=ksq[WK][0], r_sq=ksq[WK][1], rs=krs[WK][0], r_rs=krs[WK][1])
                    P.op("dve", lambda e: e.scalar_tensor_tensor(out=t1[:], in0=aps[:], scalar=vcol(V_GKR), in1=ctab[:], op0=ALU.mult, op1=ALU.mult),
                         reads=[r_aps, r_tab, R_vec], writes=[r_t1])
                    P.op("dve", lambda e: e.scalar_tensor_tensor(out=t2[:], in0=bps[:], scalar=dcol(DV_GKSS), in1=stab[:], op0=ALU.mult, op1=ALU.mult),
                         reads=[r_bps, r_tab, R_dvec], writes=[r_t2])
                    yield from norm_chain([ritem], psr, ones, R_ones, 1.0 / 128.0, None, None, EPS)
                    P.op("pool", lambda e: e.tensor_tensor(out=t1[:], in0=t1[:], in1=t2[:], op=ALU.add), reads=[r_t1, r_t2], writes=[r_t1])
                    ko, r_ko = kout[WK]
                    P.op("dve", lambda e: e.tensor_tensor(out=ko[:], in0=t1[:], in1=ritem["rs"][:], op=ALU.mult), reads=[r_t1, ritem["r_rs"]], writes=[r_ko])
                    P.dma("pool", "kst%d" % WK, lambda e: e.dma_start(out=KR[:, t * TT:(t + 1) * TT], in_=ko[:]), reads=[r_ko], writes=[R_KR])
                    if t + 1 < nA1:
                        yield from rms_tile(x_all, (t + 1) * TT, xring, "xA1_", ssr[(t + 1) % 2][0], ssr[(t + 1) % 2][1], hbr, tpr,
                                            hTr[(t + 1) % 2][0], hTr[(t + 1) % 2][1])

                run_rr([lru_window(0), v_gen(), k_gen()])
        P.barrier()

        with contextlib.ExitStack() as st:
            def T(name, shape, dt):
                return st.enter_context(nc.sbuf_tensor(name, shape, dt))
            stage = Ring(st, nc, "stgA2_", 2, [128, 1024], F32)
            wA2, r_wA2 = load_scaled_weight(st, "wA2", winA2, 8, 1280, V_GMIX, stage, "stg")
            wQN, r_wQN = load_scaled_weight(st, "wQN", wqN, 2, 1024, V_GQL, stage, "stg")
            wQR, r_wQR = load_scaled_weight(st, "wQR", wqR, 2, 512, V_GQL, stage, "stg")
            wQS, r_wQS = load_scaled_weight(st, "wQS", wqS, 2, 512, V_GQL, stage, "stg")
            ones2 = T("ones2", [128, 128], BF16)
            r_ones2 = Res()
            P.op("pool", lambda e: e.memset(ones2[:], 0.0), writes=[r_ones2])
            P.op("pool", lambda e: e.memset(ones2[0:64, 0:64], 1.0), reads=[r_ones2], writes=[r_ones2])
            P.op("pool", lambda e: e.memset(ones2[64:128, 64:128], 1.0), reads=[r_ones2], writes=[r_ones2])

            xring = Ring(st, nc, "xA2_", 4, [128, D], F32)
            ssr = [(T("ssA2_%d" % i, [128, 4], F32), [Res() for _ in range(4)]) for i in range(2)]
            hbr = Ring(st, nc, "hbA2_", 4, [128, D], BF16)
            hTr = [(T("hTA2_%d" % i, [128, 8, TT], BF16), [Res() for _ in range(4)]) for i in range(2)]
            tpr = Ring(st, nc, "tpA2_", 1, [128, D], BF16, psum=True)
            psr = Ring(st, nc, "psA2_", 7, [128, TT], F32, psum=True)
            cqT = T("cqT", [128, 2, TT], BF16)
            r_cqT = Res()
            sql = [(T("sqlA2_%d" % i, [128, TT], BF16), Res()) for i in range(2)]
            eqb = T("eqb", [128, TT], BF16)
            eqb64 = T("eqb64", [128, TT], BF16)
            r_eqb, r_eqb64 = Res(), Res()
            W4 = 4
            qraw = [(T("qrawA2_%d" % i, [128, TT], F32), Res()) for i in range(W4)]
            qsq = [(T("qsqA2_%d" % i, [128, TT], BF16), Res()) for i in range(W4)]
            qrs = [(T("qrsA2_%d" % i, [128, TT], F32), Res()) for i in range(W4)]
            qout = [(T("qoutA2_%d" % i, [128, TT], BF16), Res()) for i in range(W4)]
            qt1 = [(T("qt1A2_%d" % i, [128, TT], F32), Res()) for i in range(W4)]
            qt2 = qraw
            tabr = {"i": Ring(st, nc, "tbi2_", 1, [128, TT], I32), "a": Ring(st, nc, "tba2_", 1, [128, TT], F32),
                    "k": Ring(st, nc, "tbk2_", 1, [128, TT], F32)}
            ctab = T("ctabA2", [128, TT], F32)
            stab = T("stabA2", [128, TT], F32)
            r_tab = Res()
            H0 = [(T("H0_%d" % i, [128, 8, TT], F32), Res()) for i in range(1)]
            H1 = [(T("H1_%d" % i, [128, 8, TT], F32), Res()) for i in range(1)]
            gg = [(T("gg%d" % i, [128, TT], F32), Res()) for i in range(W4)]
            gu = [(T("gu%d" % i, [128, TT], F32), Res()) for i in range(W4)]
            gh = [(T("gh%d" % i, [128, TT], F32), Res()) for i in range(W4)]
            yoW = T("yoW", [128, W4, TT], BF16)
            r_yo = [Res() for _ in range(W4)]

            nA2 = lim("A2", NT_OWN)
            if nA2 > 0:
                drain(rms_tile(x_own, 0, xring, "xA2_", ssr[0][0], ssr[0][1], hbr, tpr, hTr[0][0], hTr[0][1]))
            for j in range(nA2):
                hT, r_hT = hTr[j % 2]
                h0t, r_h0 = H0[0]
                h1t, r_h1 = H1[0]
                P.dma("sp", "h0ld", lambda e: e.dma_start(out=h0t[:], in_=HS[2 * j].rearrange("c p t -> p c t")),
                      reads=[R_HS], writes=[r_h0])
                P.dma("sp", "h1ld", lambda e: e.dma_start(out=h1t[:], in_=HS[2 * j + 1].rearrange("c p t -> p c t")),
                      reads=[R_HS], writes=[r_h1])
                rope_tables(tabr, pos_own, j * TT, ctab, stab, r_tab)

                def proj(c0):
                    ps, rps, _ = psr.next()
                    P.op("pe", lambda e: mm(e, ps[:], [(wA2[:, kc, c0:c0 + 128], hT[:, kc, :]) for kc in range(8)]),
                         reads=[r_wA2] + r_hT, writes=[rps])
                    return ps, rps

                lat = [proj(kc * 128) for kc in range(2)]
                for kc, (ps, rps) in enumerate(lat):
                    P.op("act", lambda e: e.activation(out=cqT[:, kc, :], in_=ps[:], func=AF.Copy), reads=[rps], writes=[r_cqT])
                    sq, rsq = sql[kc]
                    P.op("act", lambda e: e.activation(out=sq[:], in_=ps[:], func=AF.Square), reads=[rps], writes=[rsq])
                sum_ps, r_sum, _ = psr.next()
                P.op("pe", lambda e: mm(e, sum_ps[:], [(ones[:], sq[:]) for sq, _ in sql]), reads=[R_ones] + [r for _, r in sql], writes=[r_sum])
                P.op("dve", lambda e: e.tensor_scalar(out=eqb[:], in0=sum_ps[:], scalar1=EPS / 2.0, scalar2=128.0 * EPS * EPS,
                                                      op0=ALU.mult, op1=ALU.add), reads=[r_sum], writes=[r_eqb])
                P.op("dve", lambda e: e.tensor_scalar(out=eqb64[:], in0=sum_ps[:], scalar1=EPS / 4.0, scalar2=64.0 * EPS * EPS,
                                                      op0=ALU.mult, op1=ALU.add), reads=[r_sum], writes=[r_eqb64])

                def y_window(c0):
                    items = []
                    for w in range(W4):
                        c = c0 + w
                        gps, r_gps = proj(256 + c * 128)
                        items.append(dict(c=c, w=w))
                        g_, r_g = gg[w]
                        P.op("act", lambda e: e.activation(out=g_[:], in_=gps[:], func=AF.Copy), reads=[r_gps], writes=[r_g])
                    yield
                    for it in items:
                        g_, r_g = gg[it["w"]]
                        u_, r_u = gu[it["w"]]
                        P.op("pool", lambda e: e.tensor_tensor(out=u_[:], in0=g_[:], in1=g_[:], op=ALU.mult), reads=[r_g], writes=[r_u])
                    yield
                    for it in items:
                        u_, r_u = gu[it["w"]]
                        P.op("dve", lambda e: e.tensor_scalar(out=u_[:], in0=u_[:], scalar1=0.044715, scalar2=1.0, op0=ALU.mult, op1=ALU.add),
                             reads=[r_u], writes=[r_u])
                    yield
                    for it in items:
                        g_, r_g = gg[it["w"]]
                        u_, r_u = gu[it["w"]]
                        P.op("pool", lambda e: e.tensor_tensor(out=u_[:], in0=u_[:], in1=g_[:], op=ALU.mult), reads=[r_g, r_u], writes=[r_u])
                    yield
                    for it in items:
                        u_, r_u = gu[it["w"]]
                        P.op("act", lambda e: e.activation(out=u_[:], in_=u_[:], func=AF.Sigmoid, scale=1.5957691216057308),
                             reads=[r_u], writes=[r_u])
                    yield
                    for it in items:
                        c = it["c"]
                        hh, r_hh = gh[it["w"]]
                        P.op("dve", lambda e: e.tensor_scalar(out=hh[:], in0=h0t[:, c, :], scalar1=vcol(V_M0), scalar2=None, op0=ALU.mult),
                             reads=[r_h0, R_vec], writes=[r_hh])
                    yield
                    for it in items:
                        c = it["c"]
                        hh, r_hh = gh[it["w"]]
                        P.op("dve", lambda e: e.scalar_tensor_tensor(out=hh[:], in0=h1t[:, c, :], scalar=vcol(V_M1), in1=hh[:], op0=ALU.mult, op1=ALU.add),
                             reads=[r_h1, r_hh, R_vec], writes=[r_hh])
                    yield
                    for it in items:
                        g_, r_g = gg[it["w"]]
                        hh, r_hh = gh[it["w"]]
                        P.op("pool", lambda e: e.tensor_tensor(out=hh[:], in0=hh[:], in1=g_[:], op=ALU.mult), reads=[r_g, r_hh], writes=[r_hh])
                    yield
                    for it in items:
                        w = it["w"]
                        u_, r_u = gu[w]
                        hh, r_hh = gh[w]
                        P.op("dve", lambda e: e.tensor_tensor(out=yoW[:, w, :], in0=hh[:], in1=u_[:], op=ALU.mult), reads=[r_hh, r_u], writes=[r_yo[w]])
                    yield
                    P.dma("pool", "ybst", lambda e: e.dma_start(out=YB[c0:c0 + W4, :, j * TT:(j + 1) * TT].rearrange("c p t -> p c t"), in_=yoW[:]),
                          reads=r_yo, writes=[R_YB])


                def q_window(h0):
                    items = []
                    for w in range(W4):
                        h = h0 + w
                        qps, r_qps, _ = psr.next()
                        P.op("pe", lambda e: mm(e, qps[:], [(wQN[:, kc, h * 128:(h + 1) * 128], cqT[:, kc, :]) for kc in range(2)]),
                             reads=[r_wQN, r_cqT], writes=[r_qps])
                        it = dict(h=h, w=w, sq=qsq[w][0], r_sq=qsq[w][1], rs=qrs[w][0], r_rs=qrs[w][1])
                        items.append(it)
                        qr_, r_qr = qraw[w]
                        P.op("dve", lambda e: e.tensor_copy(out=qr_[:], in_=qps[:]), reads=[r_qps], writes=[r_qr])
                        P.op("act", lambda e: e.activation(out=it["sq"][:], in_=qps[:], func=AF.Square), reads=[r_qps], writes=[it["r_sq"]])
                    yield
                    yield from norm_chain(items, psr, ones, R_ones, 1.0 / 128.0, eqb, r_eqb, None, do_square=False)
                    for it in items:
                        w, h = it["w"], it["h"]
                        qr_, r_qr = qraw[w]
                        qo, r_qo = qout[w]
                        P.op("dve", lambda e: e.scalar_tensor_tensor(out=qo[:], in0=qr_[:], scalar=vcol(V_GQN), in1=it["rs"][:],
                                                                     op0=ALU.mult, op1=ALU.mult),
                             reads=[r_qr, it["r_rs"], R_vec], writes=[r_qo])
                        P.dma("pool", "qst%d" % w, lambda e: e.dma_start(out=QN[h, :, j * TT:(j + 1) * TT], in_=qo[:]),
                              reads=[r_qo], writes=[R_QN])
                    yield

                def q_gen():
                    yield from q_window(0)
                    yield from q_window(4)
                    items = []
                    for hp in range(NH // 2):
                        aps, r_aps, _ = psr.next()
                        P.op("pe", lambda e: mm(e, aps[:], [(wQR[:, kc, hp * 128:(hp + 1) * 128], cqT[:, kc, :]) for kc in range(2)]),
                             reads=[r_wQR, r_cqT], writes=[r_aps])
                        it = dict(hp=hp, w=hp, sq=qsq[hp][0], r_sq=qsq[hp][1], rs=qrs[hp][0], r_rs=qrs[hp][1])
                        items.append(it)
                        a1, r_a1 = qt1[hp]
                        P.op("dve", lambda e: e.scalar_tensor_tensor(out=a1[:], in0=aps[:], scalar=vcol(V_GQR), in1=ctab[:], op0=ALU.mult, op1=ALU.mult),
                             reads=[r_aps, r_tab, R_vec], writes=[r_a1])
                        P.op("act", lambda e: e.activation(out=it["sq"][:], in_=aps[:], func=AF.Square), reads=[r_aps], writes=[it["r_sq"]])
                    yield
                    yield from norm_chain(items, psr, ones2, r_ones2, 1.0 / 64.0, eqb64, r_eqb64, None, do_square=False)
                    for it in items:
                        hp = it["hp"]
                        bps, r_bps, _ = psr.next()
                        P.op("pe", lambda e: mm(e, bps[:], [(wQS[:, kc, hp * 128:(hp + 1) * 128], cqT[:, kc, :]) for kc in range(2)]),
                             reads=[r_wQS, r_cqT], writes=[r_bps])
                        a2, r_a2 = qt2[it["w"]]
                        P.op("dve", lambda e: e.scalar_tensor_tensor(out=a2[:], in0=bps[:], scalar=dcol(DV_GQSS), in1=stab[:], op0=ALU.mult, op1=ALU.mult),
                             reads=[r_bps, r_tab, R_dvec], writes=[r_a2])
                    yield
                    for it in items:
                        a1, r_a1 = qt1[it["w"]]
                        a2, r_a2 = qt2[it["w"]]
                        P.op("pool", lambda e: e.tensor_tensor(out=a1[:], in0=a1[:], in1=a2[:], op=ALU.add), reads=[r_a1, r_a2], writes=[r_a1])
                    yield
                    for it in items:
                        w, hp = it["w"], it["hp"]
                        a1, r_a1 = qt1[w]
                        qo, r_qo = qout[w]
                        P.op("dve", lambda e: e.tensor_tensor(out=qo[:], in0=a1[:], in1=it["rs"][:], op=ALU.mult), reads=[r_a1, it["r_rs"]], writes=[r_qo])
                        P.dma("pool", "qst%d" % w, lambda e: e.dma_start(out=QR[hp, :, j * TT:(j + 1) * TT], in_=qo[:]), reads=[r_qo], writes=[R_QR])
                    yield
                    if j + 1 < nA2:
                        yield from rms_tile(x_own, (j + 1) * TT, xring, "xA2_", ssr[(j + 1) % 2][0], ssr[(j + 1) % 2][1], hbr, tpr,
                                 hTr[(j + 1) % 2][0], hTr[(j + 1) % 2][1])

                def y_gen():
                    yield from y_window(0)
                    yield from y_window(4)

                drain(y_gen())
                drain(q_gen())
        P.barrier()

        with contextlib.ExitStack() as st:
            def T(name, shape, dt):
                return st.enter_context(nc.sbuf_tensor(name, shape, dt))
            KrL = T("KrL", [128, S], BF16)
            KrH = T("KrH", [128, S], BF16)
            r_KrL, r_KrH = Res(), Res()
            MK = T("MK", [128, 8, TT], BF16)
            r_MK = Res()
            P.dma("sp", "krld", lambda e: e.dma_start(out=KrL[:], in_=KR), reads=[R_KR], writes=[r_KrL])
            P.dma("sp", "krld2", lambda e: e.dma_start(out=KrH[:], in_=KR), reads=[R_KR], writes=[r_KrH])
            P.op("pool", lambda e: e.memset(KrL[64:128, :], 0.0), reads=[r_KrL], writes=[r_KrL])
            P.op("pool", lambda e: e.memset(KrH[0:64, :], 0.0), reads=[r_KrH], writes=[r_KrH])
            P.dma("pool", "mkld", lambda e: e.dma_start(out=MK[:], in_=masks.rearrange("m p t -> p m t")), writes=[r_MK])
            Kt = [(T("Kt%d" % i, [128, S], BF16), Res()) for i in range(2)]
            Vt = [(T("Vt%d" % i, [128, S], BF16), Res()) for i in range(2)]
            Qn = [(T("Qn%d" % i, [128, S_OWN], BF16), Res()) for i in range(2)]
            Qr = [(T("Qr%d" % i, [128, S_OWN], BF16), Res()) for i in range(2)]
            spr = Ring(st, nc, "spsB_", 4, [128, TT], F32, psum=True)
            opr = Ring(st, nc, "opsB_", 2, [128, TT], F32, psum=True)
            lpr = Ring(st, nc, "lpsB_", 2, [128, TT], F32, psum=True)
            ptr = Ring(st, nc, "ptB_", 6, [128, TT], BF16)
            rlr = Ring(st, nc, "rlB_", 2, [128, TT], F32)
            yor = Ring(st, nc, "yoB_", 2, [128, TT], BF16)
            accr = Ring(st, nc, "accB_", 2, [128, TT], F32)
            onesf = T("onesf", [128, 128], F32)
            r_onesf = Res()
            P.op("pool", lambda e: e.memset(onesf[:], 1.0), writes=[r_onesf])

            def load_head(h):
                kt, r_kt = Kt[h % 2]
                vt, r_vt = Vt[h % 2]
                qn, r_qn = Qn[h % 2]
                P.dma("sp", "kld%d" % (h % 2), lambda e: e.dma_start(out=kt[:], in_=KT[h]), reads=[R_KT], writes=[r_kt])
                P.dma("sp", "vld%d" % (h % 2), lambda e: e.dma_start(out=vt[:], in_=VS[h]), reads=[R_VS], writes=[r_vt])
                P.dma("sp", "qld%d" % (h % 2), lambda e: e.dma_start(out=qn[:], in_=QN[h]), reads=[R_QN], writes=[r_qn])
                if h % 2 == 0:
                    qr, r_qr = Qr[(h // 2) % 2]
                    P.dma("sp", "qrld%d" % ((h // 2) % 2), lambda e: e.dma_start(out=qr[:], in_=QR[h // 2]), reads=[R_QR], writes=[r_qr])

            load_head(0)
            for h in range(lim("B", NH)):
                if h + 1 < NH:
                    load_head(h + 1)
                kt, r_kt = Kt[h % 2]
                vt, r_vt = Vt[h % 2]
                qn, r_qn = Qn[h % 2]
                qr, r_qr = Qr[(h // 2) % 2]
                Krh, r_Krh = (KrL, r_KrL) if h % 2 == 0 else (KrH, r_KrH)
                for j in range(NT_OWN):
                    nkb = 8 * j + 8
                    q0 = j * TT
                    ops_, r_ops, _ = opr.next()
                    lps_, r_lps, _ = lpr.next()
                    acc, r_acc, _ = accr.next()
                    sbufs = {}

                    def issue_S(kb):
                        sp_, r_sp, _ = spr.next()
                        sbufs[kb] = (sp_, r_sp)
                        P.op("pe", lambda e, sp_=sp_, kb=kb: mm(e, sp_[:], [
                            (kt[:, kb * 128:(kb + 1) * 128], qn[:, q0:q0 + TT]),
                            (Krh[:, kb * 128:(kb + 1) * 128], qr[:, q0:q0 + TT])]),
                            reads=[r_kt, r_qn, r_Krh, r_qr], writes=[r_sp])

                    issue_S(0)
                    if nkb > 1:
                        issue_S(1)
                    for kb in range(nkb):
                        sp_, r_sp = sbufs.pop(kb)
                        pt, r_pt, _ = ptr.next()
                        P.op("act", lambda e, sp_=sp_, pt=pt: e.activation(out=pt[:], in_=sp_[:], func=AF.Exp, scale=ATT_SCALE),
                             reads=[r_sp], writes=[r_pt])
                        mi = kb - (nkb - 8)
                        if mi >= 0:
                            P.op("dve", lambda e, pt=pt, mi=mi: e.tensor_tensor(out=pt[:], in0=pt[:], in1=MK[:, mi, :], op=ALU.mult),
                                 reads=[r_pt, r_MK], writes=[r_pt])
                        if kb + 2 < nkb:
                            issue_S(kb + 2)

                        if kb % 3 != 1:
                            def pv(e):
                                e.matmul(ops_[:], lhsT=vt[:, kb * 128:(kb + 1) * 128], rhs=pt[:], start=(kb == 0), stop=(kb == nkb - 1))
                                e.matmul(lps_[:], lhsT=ones[:], rhs=pt[:], start=(kb == 0), stop=False)
                            P.op("pe", pv, reads=[r_vt, r_pt, R_ones], writes=[r_ops, r_lps])
                        else:
                            if kb == 1:
                                P.op("dve", lambda e: e.tensor_copy(out=acc[:], in_=pt[:]), reads=[r_pt], writes=[r_acc])
                            else:
                                P.op("dve", lambda e: e.tensor_tensor(out=acc[:], in0=acc[:], in1=pt[:], op=ALU.add), reads=[r_pt, r_acc], writes=[r_acc])
                            P.op("pe", lambda e: e.matmul(ops_[:], lhsT=vt[:, kb * 128:(kb + 1) * 128], rhs=pt[:], start=(kb == 0), stop=(kb == nkb - 1)),
                                 reads=[r_vt, r_pt], writes=[r_ops])
                    P.op("pe", lambda e: e.matmul(lps_[:], lhsT=onesf[:], rhs=acc[:], start=False, stop=True), reads=[r_onesf, r_acc], writes=[r_lps])
                    rl, r_rl, _ = rlr.next()
                    P.op("dve", lambda e, rl=rl, lps_=lps_: e.reciprocal(out=rl[:], in_=lps_[:]), reads=[r_lps], writes=[r_rl])
                    yo, r_yo, yi = yor.next()
                    P.op("dve", lambda e, yo=yo, ops_=ops_, rl=rl: e.tensor_tensor(out=yo[:], in0=ops_[:], in1=rl[:], op=ALU.mult),
                         reads=[r_ops, r_rl], writes=[r_yo])
                    P.dma("pool", "yast%d" % yi, lambda e, yo=yo, h=h, q0=q0: e.dma_start(out=YA[h, :, q0:q0 + TT], in_=yo[:]),
                          reads=[r_yo], writes=[R_YA])
        P.barrier()

        with contextlib.ExitStack() as st:
            def T(name, shape, dt):
                return st.enter_context(nc.sbuf_tensor(name, shape, dt))
            stage = Ring(st, nc, "stgC1_", 2, [128, 1024], F32)
            wC, r_wC = load_scaled_weight(st, "wC", winC, 8, 2048, V_GMIX, stage, "stg")
            wPA, r_wPA = load_cast_weight(st, "wPA", wpa, 8, D, "wcast")
            wPL, r_wPL = load_cast_weight(st, "wPL", wpl, 8, D, "wcast")
            wO, r_wO = load_cast_weight(st, "wO", wo, 8, D, "wcast")
            xring = Ring(st, nc, "xC1_", 8, [128, D], F32)
            ssr = [(T("ssC1_%d" % i, [128, 4], F32), [Res() for _ in range(4)]) for i in range(2)]
            hbr = Ring(st, nc, "hbC1_", 4, [128, D], BF16)
            hTr = [(T("hTC1_%d" % i, [128, 8, TT], BF16), [Res() for _ in range(4)]) for i in range(2)]
            tpr = Ring(st, nc, "tpC1_", 2, [128, D], BF16, psum=True)
            psr = Ring(st, nc, "psC1_", 6, [128, TT], F32, psum=True)
            yaT = [(T("yaT%d" % i, [128, 8, TT], BF16), Res()) for i in range(2)]
            ybT = [(T("ybT%d" % i, [128, 8, TT], BF16), Res()) for i in range(2)]
            sga = Ring(st, nc, "sga_", 2, [128, TT], F32)
            sgb = Ring(st, nc, "sgb_", 2, [128, TT], F32)
            mT = [(T("mT%d" % i, [128, 8, TT], BF16), [Res() for _ in range(8)]) for i in range(2)]

            nC1 = lim("C1", NT_OWN)
            xs_next = []
            if nC1 > 0:
                drain(rms_tile(x_own, 0, xring, "xC1_", ssr[0][0], ssr[0][1], hbr, tpr, hTr[0][0], hTr[0][1], xs_out=xs_next))
            for j in range(nC1):
                hT, r_hT = hTr[j % 2]
                ya, r_ya = yaT[j % 2]
                yb, r_yb = ybT[j % 2]
                mt, r_mt = mT[j % 2]
                P.dma("sp", "yald%d" % (j % 2), lambda e, ya=ya, j=j: e.dma_start(
                    out=ya[:], in_=YA[:, :, j * TT:(j + 1) * TT].rearrange("h p t -> p h t")), reads=[R_YA], writes=[r_ya])
                P.dma("sp", "ybld%d" % (j % 2), lambda e, yb=yb, j=j: e.dma_start(
                    out=yb[:], in_=YB[:, :, j * TT:(j + 1) * TT].rearrange("c p t -> p c t")), reads=[R_YB], writes=[r_yb])
                xs = xs_next
                for m in range(8):
                    if m == 4 and j + 1 < nC1:
                        xs_next = []
                        drain(rms_tile(x_own, (j + 1) * TT, xring, "xC1_", ssr[(j + 1) % 2][0], ssr[(j + 1) % 2][1], hbr, tpr,
                                       hTr[(j + 1) % 2][0], hTr[(j + 1) % 2][1], xs_out=xs_next))
                    gaps, r_gaps, _ = psr.next()
                    gbps, r_gbps, _ = psr.next()
                    P.op("pe", lambda e, gaps=gaps, m=m: mm(e, gaps[:], [(wC[:, kc, m * 128:(m + 1) * 128], hT[:, kc, :]) for kc in range(8)]),
                         reads=[r_wC] + r_hT, writes=[r_gaps])
                    P.op("pe", lambda e, gbps=gbps, m=m: mm(e, gbps[:], [(wC[:, kc, 1024 + m * 128:1024 + (m + 1) * 128], hT[:, kc, :]) for kc in range(8)]),
                         reads=[r_wC] + r_hT, writes=[r_gbps])
                    sa, r_sa, _ = sga.next()
                    sb, r_sb, _ = sgb.next()
                    P.op("act", lambda e, gaps=gaps, sa=sa: e.activation(out=sa[:], in_=gaps[:], func=AF.Sigmoid), reads=[r_gaps], writes=[r_sa])
                    P.op("act", lambda e, gbps=gbps, sb=sb: e.activation(out=sb[:], in_=gbps[:], func=AF.Sigmoid), reads=[r_gbps], writes=[r_sb])
                    paps, r_paps, _ = psr.next()
                    plps, r_plps, _ = psr.next()
                    P.op("pe", lambda e, paps=paps, m=m: mm(e, paps[:], [(wPA[:, kc, m * 128:(m + 1) * 128], ya[:, kc, :]) for kc in range(8)]),
                         reads=[r_wPA, r_ya], writes=[r_paps])
                    P.op("pe", lambda e, plps=plps, m=m: mm(e, plps[:], [(wPL[:, kc, m * 128:(m + 1) * 128], yb[:, kc, :]) for kc in range(8)]),
                         reads=[r_wPL, r_yb], writes=[r_plps])
                    P.op("dve", lambda e, sa=sa, paps=paps: e.tensor_tensor(out=sa[:], in0=paps[:], in1=sa[:], op=ALU.mult),
                         reads=[r_paps, r_sa], writes=[r_sa])
                    P.op("dve", lambda e, sb=sb, plps=plps: e.tensor_tensor(out=sb[:], in0=plps[:], in1=sb[:], op=ALU.mult),
                         reads=[r_plps, r_sb], writes=[r_sb])
                    P.op("dve", lambda e, sa=sa, sb=sb, m=m: e.tensor_tensor(out=mt[:, m, :], in0=sa[:], in1=sb[:], op=ALU.add),
                         reads=[r_sa, r_sb], writes=[r_mt[m]])
                for s in range(4):
                    xt, rx, si = xs[s]
                    pss = [psr.next() for _ in range(2)]

                    def f(e):
                        for kc in range(8):
                            for half in range(2):
                                e.matmul(pss[half][0][:], lhsT=mt[:, kc, s * 128:(s + 1) * 128], rhs=wO[:, kc, half * 512:(half + 1) * 512],
                                         start=(kc == 0), stop=(kc == 7))
                    P.op("pe", f, reads=r_mt + [r_wO], writes=[pss[0][1], pss[1][1]])
                    for half in range(2):
                        P.op("dve", lambda e: e.tensor_tensor(out=xt[:, half * 512:(half + 1) * 512], in0=pss[half][0][:],
                                                              in1=xt[:, half * 512:(half + 1) * 512], op=ALU.add),
                             reads=[pss[half][1], rx], writes=[rx])
                    r0 = j * TT + s * 128
                    P.dma("pool", "x1st%d" % si, lambda e, xt=xt, r0=r0: e.dma_start(out=X1[r0:r0 + 128, :], in_=xt[:]),
                          reads=[rx], writes=[R_X1])
        P.barrier()

        with contextlib.ExitStack() as st:
            def T(name, shape, dt):
                return st.enter_context(nc.sbuf_tensor(name, shape, dt))
            wG = T("wG", [128, 8, FF], BF16)
            wU = T("wU", [128, 8, FF], BF16)
            wD, r_wD = load_cast_weight(st, "wD", wfd, NF, D, "wcast")
            with contextlib.ExitStack() as st2:
                stage = Ring(st2, nc, "stgC2_", 2, [128, 1024], F32)
                _, r_wG = load_scaled_weight(st, "wG", wfg, 8, FF, V_GFFN, stage, "stg", w=wG)
                _, r_wU = load_scaled_weight(st, "wU", wfu, 8, FF, V_GFFN, stage, "stg", w=wU)
            P.barrier()
            xring = Ring(st, nc, "xC2_", 4, [128, D], F32)
            xrr = Ring(st, nc, "xrC2_", 2, [128, D], F32)
            ssr = [(T("ssC2_%d" % i, [128, 4], F32), [Res() for _ in range(4)]) for i in range(2)]
            hbr = Ring(st, nc, "hbC2_", 4, [128, D], BF16)
            hTr = [(T("hTC2_%d" % i, [128, 8, TT], BF16), [Res() for _ in range(4)]) for i in range(1)]
            tpr = Ring(st, nc, "tpC2_", 2, [128, D], BF16, psum=True)
            psr = Ring(st, nc, "psC2_", 6, [128, TT], F32, psum=True)
            aT = T("aT", [128, NF, TT], BF16)
            r_aT = [Res() for _ in range(NF)]
            slr = Ring(st, nc, "slr_", 2, [128, TT], F32)

            nC2 = lim("C2", NT_OWN)
            hT, r_hT = hTr[0]
            if nC2 > 0:
                drain(rms_tile(X1, 0, xring, "xC2_", ssr[0][0], ssr[0][1], hbr, tpr, hT, r_hT, rd=[R_X1]))
            for j in range(nC2):
                for m in range(NF):
                    gps, r_gps, _ = psr.next()
                    ups, r_ups, _ = psr.next()
                    P.op("pe", lambda e: mm(e, gps[:], [(wG[:, kc, m * 128:(m + 1) * 128], hT[:, kc, :]) for kc in range(8)]),
                         reads=[r_wG] + r_hT, writes=[r_gps])
                    P.op("pe", lambda e: mm(e, ups[:], [(wU[:, kc, m * 128:(m + 1) * 128], hT[:, kc, :]) for kc in range(8)]),
                         reads=[r_wU] + r_hT, writes=[r_ups])
                    sl, r_sl, _ = slr.next()
                    P.op("act", lambda e: e.activation(out=sl[:], in_=gps[:], func=AF.Silu), reads=[r_gps], writes=[r_sl])
                    P.op("dve", lambda e: e.tensor_tensor(out=aT[:, m, :], in0=ups[:], in1=sl[:], op=ALU.mult),
                         reads=[r_ups, r_sl], writes=[r_aT[m]])
                if j + 1 < nC2:
                    drain(rms_tile(X1, (j + 1) * TT, xring, "xC2_", ssr[(j + 1) % 2][0], ssr[(j + 1) % 2][1], hbr, tpr, hT, r_hT, rd=[R_X1]))
                for s in range(4):
                    xt, rx, si = xrr.next()
                    r0 = j * TT + s * 128
                    P.dma("sp", "xrC2_%d" % si, lambda e: e.dma_start(out=xt[:], in_=X1[r0:r0 + 128, :]), reads=[R_X1], writes=[rx])
                    pss = [psr.next() for _ in range(2)]

                    def f(e):
                        for kc in range(NF):
                            for half in range(2):
                                e.matmul(pss[half][0][:], lhsT=aT[:, kc, s * 128:(s + 1) * 128], rhs=wD[:, kc, half * 512:(half + 1) * 512],
                                         start=(kc == 0), stop=(kc == NF - 1))
                    P.op("pe", f, reads=r_aT + [r_wD], writes=[pss[0][1], pss[1][1]])
                    for half in range(2):
                        P.op("dve", lambda e: e.tensor_tensor(out=xt[:, half * 512:(half + 1) * 512], in0=pss[half][0][:],
                                                              in1=xt[:, half * 512:(half + 1) * 512], op=ALU.add),
                             reads=[pss[half][1], rx], writes=[rx])
                    od = P.dma("pool", "ost%d" % si, lambda e: e.dma_start(out=out[r0:r0 + 128, :], in_=xt[:]), reads=[rx])
                    out_dmas.append(od)
        P.finish(out_dmas)
        P.emit()
    return nc


_NC_CACHE = {}
_PHASES = ("A1", "A2", "B", "C1", "C2")
_DBG = ""


def _prep_inputs(x, positions, norm_mix_g, w_in, q_lat_g, w_q_up, kv_lat_g, w_kv_up, q_head_g, k_head_g,
                 conv_w, conv_b, lru_wa, lru_ba, lru_wx, lru_bx, lru_lambda, w_proj_attn, w_proj_lru,
                 w_out, norm_ffn_g, w_ffn_gate, w_ffn_up, w_ffn_down):
    f32 = np.float32
    A = lambda a: np.ascontiguousarray(np.asarray(a))
    x = A(x)
    positions = A(positions)
    w_in = A(w_in)[0]
    perm = np.concatenate([np.arange(32, 64), np.arange(0, 32)])
    kr = w_in[:, 512:576]
    sw = kr[:, perm]
    winA1 = A(np.concatenate([w_in[:, 256:512], kr, kr, sw, sw, w_in[:, 576:1600]], axis=1))
    winA2 = A(np.concatenate([w_in[:, 0:256], w_in[:, 1600:2624]], axis=1))
    winC = A(w_in[:, 2624:4672])
    wkv = A(w_kv_up)[0].reshape(256, NH, 2, 128)
    wkvK = A(wkv[:, :, 0, :].reshape(256, 1024))
    wkvV = A(wkv[:, :, 1, :].reshape(256, 1024))
    wq = A(w_q_up)[0].reshape(256, NH, 192)
    wqN = A(wq[:, :, :128].reshape(256, 1024))
    wqR = A(wq[:, :, 128:].reshape(256, 512))
    wqS = A(wq[:, :, 128:][:, :, perm].reshape(256, 512))
    wa = A(lru_wa)[0]
    wx = A(lru_wx)[0]
    lruA = np.zeros((8, 128, 128), f32)
    lruX = np.zeros((8, 128, 128), f32)
    for c in range(8):
        for b in range(2):
            lruA[c, b * 64:(b + 1) * 64, b * 64:(b + 1) * 64] = wa[2 * c + b]
            lruX[c, b * 64:(b + 1) * 64, b * 64:(b + 1) * 64] = wx[2 * c + b]

    def pc(v, n):
        return A(np.asarray(v, f32).reshape(n, 128).T)

    p = np.arange(128)
    qg = A(q_head_g)[0]
    kg = A(k_head_g)[0]
    vecs = np.zeros((128, NV), f32)
    vecs[:, V_GMIX:V_GMIX + 8] = pc(A(norm_mix_g)[0], 8)
    vecs[:, V_GFFN:V_GFFN + 8] = pc(A(norm_ffn_g)[0], 8)
    vecs[:, V_GQL:V_GQL + 2] = pc(A(q_lat_g)[0], 2)
    vecs[:, V_GKVL:V_GKVL + 2] = pc(A(kv_lat_g)[0], 2)
    vecs[:, V_GQN] = qg[:128]
    vecs[:, V_GQR] = qg[128 + (p % 64)]
    vecs[:, V_GQS] = qg[128 + ((p % 64) + 32) % 64]
    vecs[:, V_GKN] = kg[:128]
    vecs[:, V_GKR] = kg[128 + (p % 64)]
    vecs[:, V_GKS] = kg[128 + ((p % 64) + 32) % 64]
    vecs[:, V_SGN] = np.where((p % 64) < 32, -1.0, 1.0).astype(f32)
    half = 32
    inv_freq = (np.float32(10000.0) ** (-(np.arange(half, dtype=f32) / np.float32(half)))).astype(f32)
    vecs[:, V_INVF] = inv_freq[p % 32]
    cw = A(conv_w)[0]
    for c in range(8):
        for tap in range(4):
            vecs[:, V_CONVW + c * 4 + tap] = cw[tap, c * 128:(c + 1) * 128]
    vecs[:, V_CONVB:V_CONVB + 8] = pc(A(conv_b)[0], 8)
    vecs[:, V_BA:V_BA + 8] = pc(A(lru_ba)[0], 8)
    vecs[:, V_BX:V_BX + 8] = pc(A(lru_bx)[0], 8)
    vecs[:, V_LAM:V_LAM + 8] = pc(A(lru_lambda)[0], 8)

    kk = np.arange(128)[:, None]
    qq = np.arange(TT)[None, :]
    Md = [((128 * d + kk) <= qq).astype(f32) for d in range(4)]
    one = np.ones((128, TT), f32)
    zero = np.zeros((128, TT), f32)
    mask_par = {1: np.stack([one, one, one, one] + Md), 0: np.stack(Md + [zero, zero, zero, zero])}

    common = dict(winA1=winA1, winA2=winA2, winC=winC, wkvK=wkvK, wkvV=wkvV, wqN=wqN, wqR=wqR, wqS=wqS,
                  lruA=lruA, lruX=lruX, wpa=A(w_proj_attn)[0], wpl=A(w_proj_lru)[0], wo=A(w_out)[0],
                  wfg=A(w_ffn_gate)[0], wfu=A(w_ffn_up)[0], wfd=A(w_ffn_down)[0])
    in_maps = []
    for c in range(8):
        b, par = c // 2, c % 2
        xb = x[b]
        v = vecs.copy()
        v[:, V_M0] = 1.0 if par == 0 else 0.0
        v[:, V_M1] = 1.0 if par == 1 else 0.0
        m = dict(common)
        m["x_all"] = xb
        m["x_own"] = A(xb.reshape(NT_ALL, TT, D)[par::2].reshape(S_OWN, D))
        m["pos_all"] = A(positions[b].reshape(1, S).astype(np.int32))
        m["pos_own"] = A(positions[b].reshape(NT_ALL, TT)[par::2].reshape(1, S_OWN).astype(np.int32))
        m["vecs"] = v
        m["masks"] = mask_par[par]
        in_maps.append(m)
    return in_maps


def kernel(**inputs):
    in_maps = _prep_inputs(**inputs)
    if "nc" not in _NC_CACHE:
        _NC_CACHE["nc"] = build_program()
    nc = _NC_CACHE["nc"]
    res = run_bass_kernel_spmd(nc, in_maps, core_ids=list(range(8)))
    outp = np.empty((4, S, D), np.float32)
    for c in range(8):
        b, par = c // 2, c % 2
        o = np.asarray(res.results[c]["out"]).reshape(NT_OWN, TT, D)
        outp[b].reshape(NT_ALL, TT, D)[par::2] = o
    return outp
```

```python
import contextlib
import numpy as np
import concourse.bass as bass
import concourse.mybir as mybir
from concourse.bass_utils import run_bass_kernel_spmd

F32 = mybir.dt.float32
BF16 = mybir.dt.bfloat16
I32 = mybir.dt.int32
AF = mybir.ActivationFunctionType
ALU = mybir.AluOpType

D = 1024
S = 8192
NH = 8
TT = 512
NT_ALL = S // TT
NT_OWN = NT_ALL // 2
S_OWN = S // 2
FF = 2816
NF = FF // 128
EPS = 1e-6
ATT_SCALE = 192 ** -0.5
ENGS = ("pe", "act", "dve", "pool", "sp")


class Res:
    __slots__ = ("name", "last_w", "readers")

    def __init__(self, name=""):
        self.name = name
        self.last_w = None
        self.readers = []


class Op:
    __slots__ = ("eng", "fn", "deps", "signaled", "sig", "is_dma", "key", "cum", "dmacnt")

    def __init__(self, eng, fn):
        self.eng = eng
        self.fn = fn
        self.deps = []
        self.signaled = False
        self.sig = 0
        self.is_dma = False
        self.key = None
        self.cum = 0
        self.dmacnt = 0


class _Rec:
    def __init__(self):
        self.calls = []

    def __getattr__(self, name):
        def f(*a, **k):
            self.calls.append((name, a, k))
            return None
        return f


def _record(fn):
    if fn is None:
        return None
    r = _Rec()
    fn(r)
    calls = r.calls

    def replay(eng):
        return [getattr(eng, n)(*a, **k) for (n, a, k) in calls]
    return replay


class Prog:
    def __init__(self, nc):
        self.nc = nc
        self.ops = []
        self.dma_cum = {}
        self.last_dma = {}
        self.final_waits = []

    def _track(self, op, reads, writes):
        deps = {}
        for r in reads:
            if r.last_w is not None:
                deps[id(r.last_w)] = r.last_w
        for w in writes:
            if w.last_w is not None:
                deps[id(w.last_w)] = w.last_w
            for rd in w.readers:
                deps[id(rd)] = rd
        for d in deps.values():
            if d.is_dma:
                op.deps.append((d, self.dma_cum[d.key]))
            else:
                if d.eng == op.eng and not op.is_dma and d.eng == "pe":
                    continue
                d.signaled = True
                op.deps.append((d, None))
        for r in reads:
            r.readers.append(op)
        for w in writes:
            w.last_w = op
            w.readers = []

    def op(self, eng, fn, reads=(), writes=()):
        o = Op(eng, _record(fn))
        self._track(o, reads, writes)
        self.ops.append(o)
        return o

    def dma(self, queue, key, fn, reads=(), writes=(), n=1):
        o = Op(queue, _record(fn))
        o.is_dma = True
        o.key = key
        self._track(o, reads, writes)
        self.dma_cum[key] = self.dma_cum.get(key, 0) + 16 * n
        o.cum = self.dma_cum[key]
        o.dmacnt = n
        self.last_dma[key] = o
        self.ops.append(o)
        return o

    def barrier(self):
        last = {}
        for o in self.ops:
            if not o.is_dma and o.fn is not None:
                last[o.eng] = o
        for e in ENGS:
            b = Op(e, None)
            for e2, o in last.items():
                if e2 != e:
                    o.signaled = True
                    b.deps.append((o, None))
            for k, d in self.last_dma.items():
                b.deps.append((d, self.dma_cum[k]))
            self.ops.append(b)

    def finish(self, dma_ops):
        self.final_waits = [(d, self.dma_cum[d.key]) for d in dma_ops]

    def emit(self):
        nc = self.nc
        cnt = {e: 0 for e in ENGS}
        for o in self.ops:
            if o.is_dma:
                continue
            if o.signaled:
                cnt[o.eng] += 1
                o.sig = cnt[o.eng]
        keys = sorted(self.dma_cum.keys())
        with contextlib.ExitStack() as st:
            esem = {e: st.enter_context(nc.semaphore("S_" + e)) for e in ("pe", "act", "dve", "pool")}
            dsem = {k: st.enter_context(nc.semaphore("D_" + k)) for k in keys}
            block = st.enter_context(nc.Block())
            ops = self.ops
            final_waits = self.final_waits

            def run(engname, eng):
                waited = {}
                for o in ops:
                    if o.eng != engname:
                        continue
                    for (d, cum) in o.deps:
                        if d.is_dma:
                            s, v, kk = dsem[d.key], cum, ("d", d.key)
                        else:
                            s, v, kk = esem[d.eng], d.sig, ("e", d.eng)
                        if waited.get(kk, 0) >= v:
                            continue
                        waited[kk] = v
                        eng.wait_ge(s, v)
                    if o.fn is None:
                        continue
                    r = o.fn(eng)
                    if o.is_dma:
                        assert len(r) == o.dmacnt, (len(r), o.dmacnt, o.key)
                        for ins in r:
                            ins.then_inc(dsem[o.key], 16)
                    elif o.signaled:
                        r[-1].then_inc(esem[o.eng], 1)
                if engname == "sp":
                    for (d, cum) in final_waits:
                        eng.wait_ge(dsem[d.key], cum)

            @block.sync
            def _(e):
                run("sp", e)

            @block.tensor
            def _(e):
                run("pe", e)

            @block.scalar
            def _(e):
                run("act", e)

            @block.vector
            def _(e):
                run("dve", e)

            @block.gpsimd
            def _(e):
                run("pool", e)
        return cnt


class Ring:
    def __init__(self, st, nc, name, n, shape, dt, psum=False):
        self.items = []
        for i in range(n):
            if psum:
                t = st.enter_context(nc.psum_tensor("%s%d" % (name, i), shape, dt))
            else:
                t = st.enter_context(nc.sbuf_tensor("%s%d" % (name, i), shape, dt))
            self.items.append((t, Res("%s%d" % (name, i)), i))
        self.i = 0

    def next(self):
        it = self.items[self.i % len(self.items)]
        self.i += 1
        return it


V_GMIX, V_GFFN, V_GQL, V_GKVL = 0, 8, 16, 18
V_GQN, V_GQR, V_GQS, V_GKN, V_GKR, V_GKS, V_SGN, V_INVF = 20, 21, 22, 23, 24, 25, 26, 27
V_CONVW, V_CONVB, V_BA, V_BX, V_LAM, V_M0, V_M1 = 28, 60, 68, 76, 84, 92, 93
NV = 94
DV_CLAM, DV_CLAM2, DV_GQSS, DV_GKSS, DV_NHALF, DV_ONE, DV_EPS = 0, 8, 16, 17, 18, 22, 23
NDV = 24

C1W = 6.28125
C2W = float(2 * np.pi - 6.28125)
PI = float(np.pi)
TWO_PI = float(2 * np.pi)


def build_program(debug=False):
    PH = {p.split(":")[0]: (int(p.split(":")[1]) if ":" in p else None) for p in _PHASES}

    def lim(name, n):
        if name not in PH:
            return 0
        return n if PH[name] is None else min(n, PH[name])
    nc = bass.Bass("TRN2", target_bir_lowering=False)

    def din(name, shape, dt=F32):
        return nc.dram_tensor(name, list(shape), dt, kind="ExternalInput").ap()

    x_all = din("x_all", [S, D])
    x_own = din("x_own", [S_OWN, D])
    pos_all = din("pos_all", [1, S], I32)
    pos_own = din("pos_own", [1, S_OWN], I32)
    winA1 = din("winA1", [D, 1536])
    winA2 = din("winA2", [D, 1280])
    winC = din("winC", [D, 2048])
    wkvK = din("wkvK", [256, 1024])
    wkvV = din("wkvV", [256, 1024])
    wqN = din("wqN", [256, 1024])
    wqR = din("wqR", [256, 512])
    wqS = din("wqS", [256, 512])
    lruA = din("lruA", [8, 128, 128])
    lruX = din("lruX", [8, 128, 128])
    vecs = din("vecs", [128, NV])
    masks = din("masks", [8, 128, TT])
    wpa = din("wpa", [D, D])
    wpl = din("wpl", [D, D])
    wo = din("wo", [D, D])
    wfg = din("wfg", [D, FF])
    wfu = din("wfu", [D, FF])
    wfd = din("wfd", [FF, D])
    out = nc.dram_tensor("out", [S_OWN, D], F32, kind="ExternalOutput").ap()

    def scratch(name, shape, dt):
        return nc.dram_tensor(name, list(shape), dt).ap()

    KT = scratch("KT", [NH, 128, S], BF16)
    KR = scratch("KR", [128, S], BF16)
    VS = scratch("VS", [NH, 128, S], BF16)
    QN = scratch("QN", [NH, 128, S_OWN], BF16)
    QR = scratch("QR", [NH // 2, 128, S_OWN], BF16)
    HS = scratch("HS", [NT_ALL, 8, 128, TT], F32)
    YB = scratch("YB", [8, 128, S_OWN], BF16)
    YA = scratch("YA", [NH, 128, S_OWN], BF16)
    X1 = scratch("X1", [S_OWN, D], F32)
    R_KT, R_KR, R_VS, R_QN, R_QR, R_HS, R_YB, R_YA, R_X1 = [Res(n) for n in
                                                           ("KT", "KR", "VS", "QN", "QR", "HS", "YB", "YA", "X1")]

    P = Prog(nc)
    out_dmas = []

    with contextlib.ExitStack() as gst:
        def gt(name, shape, dt):
            return gst.enter_context(nc.sbuf_tensor(name, shape, dt))

        vec = gt("vec", [128, NV], F32)
        dvec = gt("dvec", [128, NDV], F32)
        ident = gt("ident", [128, 128], BF16)
        ones = gt("ones", [128, 128], BF16)
        R_vec, R_dvec, R_ident, R_ones = Res("vec"), Res("dvec"), Res("ident"), Res("ones")

        def vcol(c, n=1):
            return vec[:, c:c + n]

        def dcol(c, n=1):
            return dvec[:, c:c + n]

        P.dma("sp", "const", lambda e: e.dma_start(out=vec[:], in_=vecs), writes=[R_vec])
        P.op("pool", lambda e: e.memset(ident[:], 1.0), writes=[R_ident])
        P.op("pool", lambda e: e.affine_select(out=ident[:], in_=ident[:], pattern=[[-1, 128]],
                                               compare_op=ALU.is_equal, fill=0.0, base=0,
                                               channel_multiplier=1), reads=[R_ident], writes=[R_ident])
        P.op("pool", lambda e: e.memset(ones[:], 1.0), writes=[R_ones])
        P.op("act", lambda e: e.activation(out=dcol(DV_CLAM, 8), in_=vcol(V_LAM, 8), func=AF.Exp, scale=-1.0),
             reads=[R_vec], writes=[R_dvec])
        P.op("act", lambda e: e.activation(out=dcol(DV_CLAM, 8), in_=dcol(DV_CLAM, 8), func=AF.Ln, bias=1.0),
             reads=[R_dvec], writes=[R_dvec])
        P.op("dve", lambda e: e.tensor_scalar(out=dcol(DV_CLAM2, 8), in0=dcol(DV_CLAM, 8), scalar1=-16.0,
                                              scalar2=None, op0=ALU.mult), reads=[R_dvec], writes=[R_dvec])
        P.op("dve", lambda e: e.tensor_scalar(out=dcol(DV_CLAM, 8), in0=dcol(DV_CLAM, 8), scalar1=-8.0,
                                              scalar2=None, op0=ALU.mult), reads=[R_dvec], writes=[R_dvec])
        P.op("dve", lambda e: e.tensor_tensor(out=dcol(DV_GQSS), in0=vcol(V_GQS), in1=vcol(V_SGN), op=ALU.mult),
             reads=[R_vec, R_dvec], writes=[R_dvec])
        P.op("dve", lambda e: e.tensor_tensor(out=dcol(DV_GKSS), in0=vcol(V_GKS), in1=vcol(V_SGN), op=ALU.mult),
             reads=[R_vec, R_dvec], writes=[R_dvec])
        P.op("pool", lambda e: e.memset(dcol(DV_NHALF, 4), -0.5), reads=[R_dvec], writes=[R_dvec])
        P.op("pool", lambda e: e.memset(dcol(DV_ONE), 1.0), reads=[R_dvec], writes=[R_dvec])
        P.op("pool", lambda e: e.memset(dcol(DV_EPS), EPS), reads=[R_dvec], writes=[R_dvec])

        def load_scaled_weight(st, name, src, nk, ncols, gcol, stage_ring, qkey, w=None):
            if w is None:
                w = st.enter_context(nc.sbuf_tensor(name, [128, nk, ncols], BF16))
            rw = Res(name)
            CH = 1024
            for kc in range(nk):
                for c0 in range(0, ncols, CH):
                    cw = min(CH, ncols - c0)
                    stg, rs, si = stage_ring.next()
                    P.dma("sp", "%s%d" % (qkey, si),
                          lambda e, stg=stg, kc=kc, c0=c0, cw=cw: e.dma_start(
                              out=stg[:, 0:cw], in_=src[kc * 128:(kc + 1) * 128, c0:c0 + cw]),
                          writes=[rs])
                    P.op("act", lambda e, stg=stg, kc=kc, c0=c0, cw=cw: e.activation(
                        out=w[:, kc, c0:c0 + cw], in_=stg[:, 0:cw], func=AF.Identity, scale=vcol(gcol + kc)),
                        reads=[rs, R_vec], writes=[rw])
            return w, rw

        def load_cast_weight(st, name, src, nk, ncols, key):
            w = st.enter_context(nc.sbuf_tensor(name, [128, nk, ncols], BF16))
            rw = Res(name)
            for kc in range(nk):
                P.dma("pool", key, lambda e, kc=kc: e.dma_start(out=w[:, kc, :], in_=src[kc * 128:(kc + 1) * 128, :]),
                      writes=[rw])
            return w, rw

        def rms_to_hT(xt, rx, ss, r_ss, hb_ring, tp_ring, hT, r_hT_parts, s):
            hb, rhb, _ = hb_ring.next()
            sc = ss[:, s:s + 1]
            rsc = r_ss[s]
            P.op("act", lambda e: e.activation(out=hb[:], in_=xt[:], func=AF.Square, accum_out=sc),
                 reads=[rx], writes=[rhb, rsc])
            P.op("dve", lambda e: e.tensor_scalar(out=sc, in0=sc, scalar1=1.0 / D,
                                                  scalar2=EPS, op0=ALU.mult, op1=ALU.add), reads=[rsc], writes=[rsc])
            P.op("pool", lambda e: e.tensor_tensor(out=sc, in0=sc, in1=dcol(DV_NHALF),
                                                   op=ALU.pow), reads=[rsc, R_dvec], writes=[rsc])
            P.op("act", lambda e: e.activation(out=hb[:], in_=xt[:], func=AF.Identity, scale=sc),
                 reads=[rx, rsc], writes=[rhb])
            tp, rtp, _ = tp_ring.next()

            def tr(e):
                for kc in range(8):
                    e.transpose(out=tp[:, kc * 128:(kc + 1) * 128], in_=hb[:, kc * 128:(kc + 1) * 128],
                                identity=ident[:])
            P.op("pe", tr, reads=[rhb, R_ident], writes=[rtp])
            P.op("dve", lambda e: e.tensor_copy(out=hT[:, :, s * 128:(s + 1) * 128],
                                                in_=tp[:].rearrange("p (k t) -> p k t", k=8)),
                 reads=[rtp], writes=[r_hT_parts[s]])

        def mm(e, out_ap, pairs):
            r = None
            n = len(pairs)
            for i, (l, rr) in enumerate(pairs):
                r = e.matmul(out_ap, lhsT=l, rhs=rr, start=(i == 0), stop=(i == n - 1))
            return r

        def rope_tables(st_ring, pos_src, t0, ctab, stab, r_tab):
            ti, r_ti, _ = st_ring["i"].next()
            ta, r_ta, _ = st_ring["a"].next()
            tk, r_tk, _ = st_ring["k"].next()
            P.dma("sp", "pos", lambda e: e.dma_start(out=ti[:], in_=pos_src[:, t0:t0 + TT].partition_broadcast(128)),
                  writes=[r_ti])
            P.op("dve", lambda e: e.tensor_copy(out=ta[:], in_=ti[:]), reads=[r_ti], writes=[r_ta])
            P.op("dve", lambda e: e.tensor_scalar(out=ta[:], in0=ta[:], scalar1=vcol(V_INVF), scalar2=None, op0=ALU.mult),
                 reads=[r_ta, R_vec], writes=[r_ta])
            P.op("dve", lambda e: e.tensor_scalar(out=ti[:], in0=ta[:], scalar1=1.0 / TWO_PI, scalar2=None, op0=ALU.mult),
                 reads=[r_ta], writes=[r_ti])
            P.op("dve", lambda e: e.tensor_copy(out=tk[:], in_=ti[:]), reads=[r_ti], writes=[r_tk])
            P.op("dve", lambda e: e.scalar_tensor_tensor(out=ta[:], in0=tk[:], scalar=-C1W, in1=ta[:], op0=ALU.mult,
                                                         op1=ALU.add), reads=[r_tk, r_ta], writes=[r_ta])
            P.op("dve", lambda e: e.scalar_tensor_tensor(out=ta[:], in0=tk[:], scalar=-C2W, in1=ta[:], op0=ALU.mult,
                                                         op1=ALU.add), reads=[r_tk, r_ta], writes=[r_ta])

            def fold(buf, rbuf):
                P.op("dve", lambda e: e.tensor_scalar(out=tk[:], in0=buf[:], scalar1=PI, scalar2=-TWO_PI, op0=ALU.is_gt,
                                                      op1=ALU.mult), reads=[rbuf], writes=[r_tk])
                P.op("dve", lambda e: e.tensor_tensor(out=buf[:], in0=buf[:], in1=tk[:], op=ALU.add),
                     reads=[rbuf, r_tk], writes=[rbuf])
                P.op("dve", lambda e: e.tensor_scalar(out=tk[:], in0=buf[:], scalar1=-PI, scalar2=TWO_PI, op0=ALU.is_lt,
                                                      op1=ALU.mult), reads=[rbuf], writes=[r_tk])
                P.op("dve", lambda e: e.tensor_tensor(out=buf[:], in0=buf[:], in1=tk[:], op=ALU.add),
                     reads=[rbuf, r_tk], writes=[rbuf])
            fold(ta, r_ta)
            P.op("act", lambda e: e.activation(out=stab[:], in_=ta[:], func=AF.Sin), reads=[r_ta], writes=[r_tab])
            P.op("dve", lambda e: e.tensor_scalar(out=ta[:], in0=ta[:], scalar1=PI / 2, scalar2=None, op0=ALU.add),
                 reads=[r_ta, r_tab], writes=[r_ta])
            fold(ta, r_ta)
            P.op("act", lambda e: e.activation(out=ctab[:], in_=ta[:], func=AF.Sin), reads=[r_ta], writes=[r_tab])

        def rstd_bcast(sum_ps, r_sum, scale, add_ap_or_const, r_add, outt, r_out):
            if isinstance(add_ap_or_const, float):
                P.op("dve", lambda e: e.tensor_scalar(out=outt[:], in0=sum_ps[:], scalar1=scale, scalar2=add_ap_or_const,
                                                      op0=ALU.mult, op1=ALU.add), reads=[r_sum], writes=[r_out])
            else:
                P.op("dve", lambda e: e.scalar_tensor_tensor(out=outt[:], in0=sum_ps[:], scalar=scale,
                                                             in1=add_ap_or_const, op0=ALU.mult, op1=ALU.add),
                     reads=[r_sum, r_add], writes=[r_out])
            P.op("act", lambda e: e.activation(out=outt[:], in_=outt[:], func=AF.Sqrt), reads=[r_out], writes=[r_out])
            P.op("dve", lambda e: e.reciprocal(out=outt[:], in_=outt[:]), reads=[r_out], writes=[r_out])

        def run_rr(gens):
            gens = list(gens)
            while gens:
                for g in list(gens):
                    try:
                        next(g)
                    except StopIteration:
                        gens.remove(g)

        def drain(g):
            for _ in g:
                pass

        def rms_tile(x_src, row0, xring, xkey, ss, r_ss, hbr, tpr, hT, r_hT, xs_out=None, rd=()):
            its = []
            for s in range(4):
                xt, rx, si = xring.next()
                if xs_out is not None:
                    xs_out.append((xt, rx, si))
                r0 = row0 + s * 128
                P.dma("sp", "%s%d" % (xkey, si), lambda e: e.dma_start(out=xt[:], in_=x_src[r0:r0 + 128, :]), reads=list(rd), writes=[rx])
                hb, rhb, _ = hbr.next()
                its.append((xt, rx, hb, rhb, ss[:, s:s + 1], r_ss[s]))
            yield
            for (xt, rx, hb, rhb, sc, rsc) in its:
                P.op("act", lambda e: e.activation(out=hb[:], in_=xt[:], func=AF.Square, accum_out=sc), reads=[rx], writes=[rhb, rsc])
            yield
            for (xt, rx, hb, rhb, sc, rsc) in its:
                P.op("dve", lambda e: e.tensor_scalar(out=sc, in0=sc, scalar1=1.0 / D, scalar2=EPS, op0=ALU.mult, op1=ALU.add),
                     reads=[rsc], writes=[rsc])
            yield
            for (xt, rx, hb, rhb, sc, rsc) in its:
                P.op("pool", lambda e: e.tensor_tensor(out=sc, in0=sc, in1=dcol(DV_NHALF), op=ALU.pow), reads=[rsc, R_dvec], writes=[rsc])
            yield
            for (xt, rx, hb, rhb, sc, rsc) in its:
                P.op("act", lambda e: e.activation(out=hb[:], in_=xt[:], func=AF.Identity, scale=sc), reads=[rx, rsc], writes=[rhb])
            yield
            for s, (xt, rx, hb, rhb, sc, rsc) in enumerate(its):
                tp, rtp, _ = tpr.next()

                def tr(e):
                    for kc in range(8):
                        e.transpose(out=tp[:, kc * 128:(kc + 1) * 128], in_=hb[:, kc * 128:(kc + 1) * 128], identity=ident[:])
                P.op("pe", tr, reads=[rhb, R_ident], writes=[rtp])
                P.op("dve", lambda e: e.tensor_copy(out=hT[:, :, s * 128:(s + 1) * 128], in_=tp[:].rearrange("p (k t) -> p k t", k=8)),
                     reads=[rtp], writes=[r_hT[s]])


            yield

        def norm_chain(items, psr, ones_l, r_ones_l, inv_n, add_b, r_add_b, add_const, do_square=True):
            if do_square:
                for it in items:
                    P.op("act", lambda e: e.activation(out=it["sq"][:], in_=it["ps"][:], func=AF.Square), reads=[it["r_ps"]], writes=[it["r_sq"]])
                yield
            for it in items:
                sps, r_sps, _ = psr.next()
                pairs = [(ones_l[:], it["sq"][:])]
                rd = [r_ones_l, it["r_sq"]]
                if add_b is not None:
                    pairs.append((ident[:], add_b[:]))
                    rd += [R_ident, r_add_b]
                P.op("pe", lambda e: mm(e, sps[:], pairs), reads=rd, writes=[r_sps])
                if add_b is not None:
                    P.op("act", lambda e: e.activation(out=it["rs"][:], in_=sps[:], func=AF.Ln, scale=inv_n),
                         reads=[r_sps], writes=[it["r_rs"]])
                else:
                    P.op("act", lambda e: e.activation(out=it["rs"][:], in_=sps[:], func=AF.Ln, scale=inv_n, bias=dcol(DV_EPS)),
                         reads=[r_sps, R_dvec], writes=[it["r_rs"]])
            yield
            for it in items:
                P.op("act", lambda e: e.activation(out=it["rs"][:], in_=it["rs"][:], func=AF.Exp, scale=-0.5),
                     reads=[it["r_rs"]], writes=[it["r_rs"]])
            yield

        with contextlib.ExitStack() as st:
            def T(name, shape, dt):
                return st.enter_context(nc.sbuf_tensor(name, shape, dt))
            wA1 = T("wA1", [128, 8, 1536], BF16)
            wK = T("wK", [128, 2, 1024], BF16)
            wV = T("wV", [128, 2, 1024], BF16)
            wLA, r_wLA = load_cast_weight(st, "wLA", lruA.rearrange("c p m -> (c p) m"), 8, 128, "wcast")
            wLX, r_wLX = load_cast_weight(st, "wLX", lruX.rearrange("c p m -> (c p) m"), 8, 128, "wcast")
            with contextlib.ExitStack() as st2:
                stage = Ring(st2, nc, "stgA1_", 2, [128, 1024], F32)
                _, r_wA1 = load_scaled_weight(st, "wA1", winA1, 8, 1536, V_GMIX, stage, "stg", w=wA1)
                _, r_wK = load_scaled_weight(st, "wK", wkvK, 2, 1024, V_GKVL, stage, "stg", w=wK)
                _, r_wV = load_scaled_weight(st, "wV", wkvV, 2, 1024, V_GKVL, stage, "stg", w=wV)
            P.barrier()

            xring = Ring(st, nc, "xA1_", 4, [128, D], F32)
            ssr = [(T("ssA1_%d" % i, [128, 4], F32), [Res() for _ in range(4)]) for i in range(2)]
            hbr = Ring(st, nc, "hbA1_", 4, [128, D], BF16)
            hTr = [(T("hTA1_%d" % i, [128, 8, TT], BF16), [Res() for _ in range(4)]) for i in range(2)]
            tpr = Ring(st, nc, "tpA1_", 1, [128, D], BF16, psum=True)
            psr = Ring(st, nc, "psA1_", 7, [128, TT], F32, psum=True)
            ckvT = T("ckvT", [128, 2, TT], BF16)
            r_ckvT = Res()
            sql = [(T("sqlA1_%d" % i, [128, TT], BF16), Res()) for i in range(2)]
            ekvb = T("ekvb", [128, TT], BF16)
            r_ekvb = Res()
            rvs = T("rvs", [128, 4], F32)
            r_rvs = Res()
            W4 = 8
            WK = 3
            kraw = [(T("krawA1_%d" % i, [128, TT], F32), Res()) for i in range(WK)]
            ksq = [(T("ksqA1_%d" % i, [128, TT], BF16), Res()) for i in range(WK + 1)]
            krs = [(T("krsA1_%d" % i, [128, TT], F32), Res()) for i in range(WK + 1)]
            kout = [(T("koutA1_%d" % i, [128, TT], BF16), Res()) for i in range(WK + 1)]
            vsb = Ring(st, nc, "vsbA1_", 2, [128, D], BF16)
            tabr = {"i": Ring(st, nc, "tbi_", 1, [128, TT], I32), "a": Ring(st, nc, "tba_", 1, [128, TT], F32),
                    "k": Ring(st, nc, "tbk_", 1, [128, TT], F32)}
            ctab = T("ctabA1", [128, TT], F32)
            stab = T("stabA1", [128, TT], F32)
            r_tab = Res()
            (t1, r_t1), (t2, r_t2) = kraw[0], kraw[1]
            XL = T("XL", [128, 8, TT + 3], F32)
            r_XL = [Res() for _ in range(8)]
            xcW = T("xcW", [128, W4, TT], F32)
            r_xc = [Res() for _ in range(W4)]
            lxcb = [(T("lxcb%d" % i, [128, TT], BF16), Res()) for i in range(W4)]
            lr_ = [(T("llr%d" % i, [128, TT], F32), Res()) for i in range(W4)]
            li_ = [(T("lli%d" % i, [128, TT], F32), Res()) for i in range(W4)]
            la_ = [(T("lla%d" % i, [128, TT], F32), Res()) for i in range(W4)]
            hlast = T("hlast", [128, 8], F32)
            r_hlast = [Res() for _ in range(8)]

            P.op("pool", lambda e: e.memset(XL[:], 0.0), writes=r_XL)
            P.op("pool", lambda e: e.memset(hlast[:], 0.0), writes=r_hlast)

            nA1 = lim("A1", NT_ALL)
            if nA1 > 0:
                drain(rms_tile(x_all, 0, xring, "xA1_", ssr[0][0], ssr[0][1], hbr, tpr, hTr[0][0], hTr[0][1]))
            for t in range(nA1):
                hT, r_hT = hTr[t % 2]
                rope_tables(tabr, pos_all, t * TT, ctab, stab, r_tab)

                def proj(c0):
                    ps, rps, _ = psr.next()
                    P.op("pe", lambda e: mm(e, ps[:], [(wA1[:, kc, c0:c0 + 128], hT[:, kc, :]) for kc in range(8)]),
                         reads=[r_wA1] + r_hT, writes=[rps])
                    return ps, rps

                lat = [proj(kc * 128) for kc in range(2)]
                for kc, (ps, rps) in enumerate(lat):
                    P.op("act", lambda e: e.activation(out=ckvT[:, kc, :], in_=ps[:], func=AF.Copy), reads=[rps], writes=[r_ckvT])
                    sq, rsq = sql[kc]
                    P.op("act", lambda e: e.activation(out=sq[:], in_=ps[:], func=AF.Square), reads=[rps], writes=[rsq])
                sum_ps, r_sum, _ = psr.next()
                P.op("pe", lambda e: mm(e, sum_ps[:], [(ones[:], sq[:]) for sq, _ in sql]), reads=[R_ones] + [r for _, r in sql], writes=[r_sum])
                P.op("dve", lambda e: e.tensor_scalar(out=ekvb[:], in0=sum_ps[:], scalar1=EPS / 2.0, scalar2=128.0 * EPS * EPS,
                                                      op0=ALU.mult, op1=ALU.add), reads=[r_sum], writes=[r_ekvb])
                sv_ps, r_sv, _ = psr.next()

                def svmm(e):
                    for s in range(4):
                        mm(e, sv_ps[:, s:s + 1], [(sq[:, s * 128:(s + 1) * 128], ones[:, 0:1]) for sq, _ in sql])
                P.op("pe", svmm, reads=[R_ones] + [r for _, r in sql], writes=[r_sv])
                P.op("dve", lambda e: e.tensor_scalar(out=rvs[:], in0=sv_ps[:, 0:4], scalar1=1.0 / 256.0, scalar2=EPS, op0=ALU.mult, op1=ALU.add),
                     reads=[r_sv], writes=[r_rvs])
                P.op("pool", lambda e: e.tensor_tensor(out=rvs[:], in0=rvs[:], in1=dcol(DV_NHALF, 4), op=ALU.pow),
                     reads=[r_rvs, R_dvec], writes=[r_rvs])

                def lru_window(c0):
                    items = []
                    for w in range(W4):
                        c = c0 + w
                        xps, r_xps = proj(512 + c * 128)
                        items.append(dict(c=c, w=w))
                        P.op("act", lambda e: e.activation(out=XL[:, c, 3:3 + TT], in_=xps[:], func=AF.Copy),
                             reads=[r_xps], writes=[r_XL[c]])
                        if w % 2 == 1:
                            yield
                    for p0 in range(0, W4, 2):
                        pair = items[p0:p0 + 2]
                        for tap in range(4):
                            for it in pair:
                                c, w = it["c"], it["w"]
                                cw = V_CONVW + c * 4
                                if tap == 0:
                                    P.op("dve", lambda e: e.tensor_scalar(out=xcW[:, w, :], in0=XL[:, c, 0:TT], scalar1=vcol(cw),
                                                                          scalar2=vcol(V_CONVB + c), op0=ALU.mult, op1=ALU.add),
                                         reads=[r_XL[c], R_vec], writes=[r_xc[w]])
                                else:
                                    P.op("dve", lambda e: e.scalar_tensor_tensor(out=xcW[:, w, :], in0=XL[:, c, tap:tap + TT], scalar=vcol(cw + tap),
                                                                                 in1=xcW[:, w, :], op0=ALU.mult, op1=ALU.add),
                                         reads=[r_XL[c], r_xc[w], R_vec], writes=[r_xc[w]])
                        for it in pair:
                            c, w = it["c"], it["w"]
                            P.op("pool", lambda e: e.tensor_copy(out=XL[:, c, 0:3], in_=XL[:, c, TT:TT + 3]), reads=[r_XL[c]], writes=[r_XL[c]])
                            xcb, r_xcb = lxcb[w]
                            P.op("dve", lambda e: e.tensor_copy(out=xcb[:], in_=xcW[:, w, :]), reads=[r_xc[w]], writes=[r_xcb])
                        yield
                        for it in pair:
                            c, w = it["c"], it["w"]
                            xcb, r_xcb = lxcb[w]
                            rps_, r_rps, _ = psr.next()
                            P.op("pe", lambda e: mm(e, rps_[:], [(wLA[:, c, :], xcb[:])]), reads=[r_wLA, r_xcb], writes=[r_rps])
                            lr, r_lr = lr_[w]
                            P.op("act", lambda e: e.activation(out=lr[:], in_=rps_[:], func=AF.Sigmoid, bias=vcol(V_BA + c)),
                                 reads=[r_rps, R_vec], writes=[r_lr])
                            ips_, r_ips, _ = psr.next()
                            P.op("pe", lambda e: mm(e, ips_[:], [(wLX[:, c, :], xcb[:])]), reads=[r_wLX, r_xcb], writes=[r_ips])
                            li, r_li = li_[w]
                            P.op("act", lambda e: e.activation(out=li[:], in_=ips_[:], func=AF.Sigmoid, bias=vcol(V_BX + c)),
                                 reads=[r_ips, R_vec], writes=[r_li])
                        yield
                    for it in items:
                        c, w = it["c"], it["w"]
                        lr, r_lr = lr_[w]
                        la, r_la = la_[w]
                        P.op("act", lambda e: e.activation(out=la[:], in_=lr[:], func=AF.Exp, scale=dcol(DV_CLAM + c)),
                             reads=[r_lr, R_dvec], writes=[r_la])
                    yield
                    for it in items:
                        c, w = it["c"], it["w"]
                        lr, r_lr = lr_[w]
                        P.op("act", lambda e: e.activation(out=lr[:], in_=lr[:], func=AF.Exp, scale=dcol(DV_CLAM2 + c)),
                             reads=[r_lr, R_dvec], writes=[r_lr])
                    yield
                    for it in items:
                        w = it["w"]
                        li, r_li = li_[w]
                        P.op("pool", lambda e: e.tensor_tensor(out=li[:], in0=li[:], in1=xcW[:, w, :], op=ALU.mult),
                             reads=[r_li, r_xc[w]], writes=[r_li])
                    yield
                    for it in items:
                        w = it["w"]
                        lr, r_lr = lr_[w]
                        P.op("act", lambda e: e.activation(out=lr[:], in_=lr[:], func=AF.Sqrt, scale=-1.0, bias=dcol(DV_ONE)),
                             reads=[r_lr, R_dvec], writes=[r_lr])
                    yield
                    for it in items:
                        w = it["w"]
                        li, r_li = li_[w]
                        lr, r_lr = lr_[w]
                        P.op("pool", lambda e: e.tensor_tensor(out=li[:], in0=li[:], in1=lr[:], op=ALU.mult), reads=[r_li, r_lr], writes=[r_li])
                    yield
                    for it in items:
                        c, w = it["c"], it["w"]
                        li, r_li = li_[w]
                        la, r_la = la_[w]
                        P.op("dve", lambda e: e.tensor_tensor_scan(out=xcW[:, w, :], data0=la[:], data1=li[:], initial=hlast[:, c:c + 1],
                                                                   op0=ALU.mult, op1=ALU.add),
                             reads=[r_la, r_li, r_hlast[c]], writes=[r_xc[w]])
                        P.op("act", lambda e: e.activation(out=hlast[:, c:c + 1], in_=xcW[:, w, TT - 1:TT], func=AF.Copy),
                             reads=[r_xc[w]], writes=[r_hlast[c]])
                    yield
                    P.dma("pool", "hst", lambda e: e.dma_start(out=HS[t, c0:c0 + W4].rearrange("c p t -> p c t"), in_=xcW[:]),
                          reads=r_xc, writes=[R_HS])


                def v_gen():
                    for s in range(4):
                        vt, rvt, vi = vsb.next()
                        for half in range(2):
                            ps, rps, _ = psr.next()
                            P.op("pe", lambda e: mm(e, ps[:], [(ckvT[:, kc, s * 128:(s + 1) * 128], wV[:, kc, half * 512:(half + 1) * 512])
                                                               for kc in range(2)]), reads=[r_ckvT, r_wV], writes=[rps])
                            P.op("act", lambda e: e.activation(out=vt[:, half * 512:(half + 1) * 512], in_=ps[:], func=AF.Identity,
                                                               scale=rvs[:, s:s + 1]), reads=[rps, r_rvs], writes=[rvt])
                        blk = t * 4 + s
                        P.dma("pool", "vst%d" % vi, lambda e: e.dma_start(
                            out=VS[:, :, blk * 128:(blk + 1) * 128].rearrange("h p d -> p h d"),
                            in_=vt[:].rearrange("p (h d) -> p h d", h=NH)), reads=[rvt], writes=[R_VS])
                    yield


                def k_window(h0, nh):
                    items = []
                    for w in range(nh):
                        h = h0 + w
                        kps, r_kps, _ = psr.next()
                        P.op("pe", lambda e: mm(e, kps[:], [(wK[:, kc, h * 128:(h + 1) * 128], ckvT[:, kc, :]) for kc in range(2)]),
                             reads=[r_wK, r_ckvT], writes=[r_kps])
                        it = dict(h=h, w=w, sq=ksq[w][0], r_sq=ksq[w][1], rs=krs[w][0], r_rs=krs[w][1])
                        items.append(it)
                        kr_, r_kr = kraw[w]
                        P.op("dve", lambda e: e.tensor_copy(out=kr_[:], in_=kps[:]), reads=[r_kps], writes=[r_kr])
                        P.op("act", lambda e: e.activation(out=it["sq"][:], in_=kps[:], func=AF.Square), reads=[r_kps], writes=[it["r_sq"]])
                    yield
                    yield from norm_chain(items, psr, ones, R_ones, 1.0 / 128.0, ekvb, r_ekvb, None, do_square=False)
                    for it in items:
                        w, h = it["w"], it["h"]
                        kr_, r_kr = kraw[w]
                        ko, r_ko = kout[w]
                        P.op("dve", lambda e: e.scalar_tensor_tensor(out=ko[:], in0=kr_[:], scalar=vcol(V_GKN), in1=it["rs"][:],
                                                                     op0=ALU.mult, op1=ALU.mult),
                             reads=[r_kr, it["r_rs"], R_vec], writes=[r_ko])
                        P.dma("pool", "kst%d" % w, lambda e: e.dma_start(out=KT[h, :, t * TT:(t + 1) * TT], in_=ko[:]),
                              reads=[r_ko], writes=[R_KT])
                    yield

                def k_gen():
                    yield from k_window(0, 3)
                    yield from k_window(3, 3)
                    yield from k_window(6, 2)
                    aps, r_aps = proj(256)
                    bps, r_bps = proj(384)
                    ritem = dict(ps=aps, r_ps=r_aps, sq=ksq[WK][0], r_sq=ksq[WK][1], rs=krs[WK][0], r_rs=krs[WK][1])
                    P.op("dve", lambda e: e.scalar_tensor_tensor(out=t1[:], in0=aps[:], scalar=vcol(V_GKR), in1=ctab[:], op0=ALU.mult, op1=ALU.mult),
                         reads=[r_aps, r_tab, R_vec], writes=[r_t1])
                    P.op("dve", lambda e: e.scalar_tensor_tensor(out=t2[:], in0=bps[:], scalar=dcol(DV_GKSS), in1=stab[:], op0=ALU.mult, op1=ALU.mult),
                         reads=[r_bps, r_tab, R_dvec], writes=[r_t2])
                    yield from norm_chain([ritem], psr, ones, R_ones, 1.0 / 128.0, None, None, EPS)
                    P.op("pool", lambda e: e.tensor_tensor(out=t1[:], in0=t1[:], in1=t2[:], op=ALU.add), reads=[r_t1, r_t2], writes=[r_t1])
                    ko, r_ko = kout[WK]
                    P.op("dve", lambda e: e.tensor_tensor(out=ko[:], in0=t1[:], in1=ritem["rs"][:], op=ALU.mult), reads=[r_t1, ritem["r_rs"]], writes=[r_ko])
                    P.dma("pool", "kst%d" % WK, lambda e: e.dma_start(out=KR[:, t * TT:(t + 1) * TT], in_=ko[:]), reads=[r_ko], writes=[R_KR])
                    if t + 1 < nA1:
                        yield from rms_tile(x_all, (t + 1) * TT, xring, "xA1_", ssr[(t + 1) % 2][0], ssr[(t + 1) % 2][1], hbr, tpr,
                                            hTr[(t + 1) % 2][0], hTr[(t + 1) % 2][1])

                run_rr([lru_window(0), v_gen(), k_gen()])
        P.barrier()

        with contextlib.ExitStack() as st:
            def T(name, shape, dt):
                return st.enter_context(nc.sbuf_tensor(name, shape, dt))
            stage = Ring(st, nc, "stgA2_", 2, [128, 1024], F32)
            wA2, r_wA2 = load_scaled_weight(st, "wA2", winA2, 8, 1280, V_GMIX, stage, "stg")
            wQN, r_wQN = load_scaled_weight(st, "wQN", wqN, 2, 1024, V_GQL, stage, "stg")
            wQR, r_wQR = load_scaled_weight(st, "wQR", wqR, 2, 512, V_GQL, stage, "stg")
            wQS, r_wQS = load_scaled_weight(st, "wQS", wqS, 2, 512, V_GQL, stage, "stg")
            ones2 = T("ones2", [128, 128], BF16)
            r_ones2 = Res()
            P.op("pool", lambda e: e.memset(ones2[:], 0.0), writes=[r_ones2])
            P.op("pool", lambda e: e.memset(ones2[0:64, 0:64], 1.0), reads=[r_ones2], writes=[r_ones2])
            P.op("pool", lambda e: e.memset(ones2[64:128, 64:128], 1.0), reads=[r_ones2], writes=[r_ones2])

            xring = Ring(st, nc, "xA2_", 4, [128, D], F32)
            ssr = [(T("ssA2_%d" % i, [128, 4], F32), [Res() for _ in range(4)]) for i in range(2)]
            hbr = Ring(st, nc, "hbA2_", 4, [128, D], BF16)
            hTr = [(T("hTA2_%d" % i, [128, 8, TT], BF16), [Res() for _ in range(4)]) for i in range(2)]
            tpr = Ring(st, nc, "tpA2_", 1, [128, D], BF16, psum=True)
            psr = Ring(st, nc, "psA2_", 7, [128, TT], F32, psum=True)
            cqT = T("cqT", [128, 2, TT], BF16)
            r_cqT = Res()
            sql = [(T("sqlA2_%d" % i, [128, TT], BF16), Res()) for i in range(2)]
            eqb = T("eqb", [128, TT], BF16)
            eqb64 = T("eqb64", [128, TT], BF16)
            r_eqb, r_eqb64 = Res(), Res()
            W4 = 4
            qraw = [(T("qrawA2_%d" % i, [128, TT], F32), Res()) for i in range(W4)]
            qsq = [(T("qsqA2_%d" % i, [128, TT], BF16), Res()) for i in range(W4)]
            qrs = [(T("qrsA2_%d" % i, [128, TT], F32), Res()) for i in range(W4)]
            qout = [(T("qoutA2_%d" % i, [128, TT], BF16), Res()) for i in range(W4)]
            qt1 = [(T("qt1A2_%d" % i, [128, TT], F32), Res()) for i in range(W4)]
            qt2 = qraw
            tabr = {"i": Ring(st, nc, "tbi2_", 1, [128, TT], I32), "a": Ring(st, nc, "tba2_", 1, [128, TT], F32),
                    "k": Ring(st, nc, "tbk2_", 1, [128, TT], F32)}
            ctab = T("ctabA2", [128, TT], F32)
            stab = T("stabA2", [128, TT], F32)
            r_tab = Res()
            H0 = [(T("H0_%d" % i, [128, 8, TT], F32), Res()) for i in range(1)]
            H1 = [(T("H1_%d" % i, [128, 8, TT], F32), Res()) for i in range(1)]
            gg = [(T("gg%d" % i, [128, TT], F32), Res()) for i in range(W4)]
            gu = [(T("gu%d" % i, [128, TT], F32), Res()) for i in range(W4)]
            gh = [(T("gh%d" % i, [128, TT], F32), Res()) for i in range(W4)]
            yoW = T("yoW", [128, W4, TT], BF16)
            r_yo = [Res() for _ in range(W4)]

            nA2 = lim("A2", NT_OWN)
            if nA2 > 0:
                drain(rms_tile(x_own, 0, xring, "xA2_", ssr[0][0], ssr[0][1], hbr, tpr, hTr[0][0], hTr[0][1]))
            for j in range(nA2):
                hT, r_hT = hTr[j % 2]
                h0t, r_h0 = H0[0]
                h1t, r_h1 = H1[0]
                P.dma("sp", "h0ld", lambda e: e.dma_start(out=h0t[:], in_=HS[2 * j].rearrange("c p t -> p c t")),
                      reads=[R_HS], writes=[r_h0])
                P.dma("sp", "h1ld", lambda e: e.dma_start(out=h1t[:], in_=HS[2 * j + 1].rearrange("c p t -> p c t")),
                      reads=[R_HS], writes=[r_h1])
                rope_tables(tabr, pos_own, j * TT, ctab, stab, r_tab)

                def proj(c0):
                    ps, rps, _ = psr.next()
                    P.op("pe", lambda e: mm(e, ps[:], [(wA2[:, kc, c0:c0 + 128], hT[:, kc, :]) for kc in range(8)]),
                         reads=[r_wA2] + r_hT, writes=[rps])
                    return ps, rps

                lat = [proj(kc * 128) for kc in range(2)]
                for kc, (ps, rps) in enumerate(lat):
                    P.op("act", lambda e: e.activation(out=cqT[:, kc, :], in_=ps[:], func=AF.Copy), reads=[rps], writes=[r_cqT])
                    sq, rsq = sql[kc]
                    P.op("act", lambda e: e.activation(out=sq[:], in_=ps[:], func=AF.Square), reads=[rps], writes=[rsq])
                sum_ps, r_sum, _ = psr.next()
                P.op("pe", lambda e: mm(e, sum_ps[:], [(ones[:], sq[:]) for sq, _ in sql]), reads=[R_ones] + [r for _, r in sql], writes=[r_sum])
                P.op("dve", lambda e: e.tensor_scalar(out=eqb[:], in0=sum_ps[:], scalar1=EPS / 2.0, scalar2=128.0 * EPS * EPS,
                                                      op0=ALU.mult, op1=ALU.add), reads=[r_sum], writes=[r_eqb])
                P.op("dve", lambda e: e.tensor_scalar(out=eqb64[:], in0=sum_ps[:], scalar1=EPS / 4.0, scalar2=64.0 * EPS * EPS,
                                                      op0=ALU.mult, op1=ALU.add), reads=[r_sum], writes=[r_eqb64])

                def y_window(c0):
                    items = []
                    for w in range(W4):
                        c = c0 + w
                        gps, r_gps = proj(256 + c * 128)
                        items.append(dict(c=c, w=w))
                        g_, r_g = gg[w]
                        P.op("act", lambda e: e.activation(out=g_[:], in_=gps[:], func=AF.Copy), reads=[r_gps], writes=[r_g])
                    yield
                    for it in items:
                        g_, r_g = gg[it["w"]]
                        u_, r_u = gu[it["w"]]
                        P.op("pool", lambda e: e.tensor_tensor(out=u_[:], in0=g_[:], in1=g_[:], op=ALU.mult), reads=[r_g], writes=[r_u])
                    yield
                    for it in items:
                        u_, r_u = gu[it["w"]]
                        P.op("dve", lambda e: e.tensor_scalar(out=u_[:], in0=u_[:], scalar1=0.044715, scalar2=1.0, op0=ALU.mult, op1=ALU.add),
                             reads=[r_u], writes=[r_u])
                    yield
                    for it in items:
                        g_, r_g = gg[it["w"]]
                        u_, r_u = gu[it["w"]]
                        P.op("pool", lambda e: e.tensor_tensor(out=u_[:], in0=u_[:], in1=g_[:], op=ALU.mult), reads=[r_g, r_u], writes=[r_u])
                    yield
                    for it in items:
                        u_, r_u = gu[it["w"]]
                        P.op("act", lambda e: e.activation(out=u_[:], in_=u_[:], func=AF.Sigmoid, scale=1.5957691216057308),
                             reads=[r_u], writes=[r_u])
                    yield
                    for it in items:
                        c = it["c"]
                        hh, r_hh = gh[it["w"]]
                        P.op("dve", lambda e: e.tensor_scalar(out=hh[:], in0=h0t[:, c, :], scalar1=vcol(V_M0), scalar2=None, op0=ALU.mult),
                             reads=[r_h0, R_vec], writes=[r_hh])
                    yield
                    for it in items:
                        c = it["c"]
                        hh, r_hh = gh[it["w"]]
                        P.op("dve", lambda e: e.scalar_tensor_tensor(out=hh[:], in0=h1t[:, c, :], scalar=vcol(V_M1), in1=hh[:], op0=ALU.mult, op1=ALU.add),
                             reads=[r_h1, r_hh, R_vec], writes=[r_hh])
                    yield
                    for it in items:
                        g_, r_g = gg[it["w"]]
                        hh, r_hh = gh[it["w"]]
                        P.op("pool", lambda e: e.tensor_tensor(out=hh[:], in0=hh[:], in1=g_[:], op=ALU.mult), reads=[r_g, r_hh], writes=[r_hh])
                    yield
                    for it in items:
                        w = it["w"]
                        u_, r_u = gu[w]
                        hh, r_hh = gh[w]
                        P.op("dve", lambda e: e.tensor_tensor(out=yoW[:, w, :], in0=hh[:], in1=u_[:], op=ALU.mult), reads=[r_hh, r_u], writes=[r_yo[w]])
                    yield
                    P.dma("pool", "ybst", lambda e: e.dma_start(out=YB[c0:c0 + W4, :, j * TT:(j + 1) * TT].rearrange("c p t -> p c t"), in_=yoW[:]),
                          reads=r_yo, writes=[R_YB])


                def q_window(h0):
                    items = []
                    for w in range(W4):
                        h = h0 + w
                        qps, r_qps, _ = psr.next()
                        P.op("pe", lambda e: mm(e, qps[:], [(wQN[:, kc, h * 128:(h + 1) * 128], cqT[:, kc, :]) for kc in range(2)]),
                             reads=[r_wQN, r_cqT], writes=[r_qps])
                        it = dict(h=h, w=w, sq=qsq[w][0], r_sq=qsq[w][1], rs=qrs[w][0], r_rs=qrs[w][1])
                        items.append(it)
                        qr_, r_qr = qraw[w]
                        P.op("dve", lambda e: e.tensor_copy(out=qr_[:], in_=qps[:]), reads=[r_qps], writes=[r_qr])
                        P.op("act", lambda e: e.activation(out=it["sq"][:], in_=qps[:], func=AF.Square), reads=[r_qps], writes=[it["r_sq"]])
                    yield
                    yield from norm_chain(items, psr, ones, R_ones, 1.0 / 128.0, eqb, r_eqb, None, do_square=False)
                    for it in items:
                        w, h = it["w"], it["h"]
                        qr_, r_qr = qraw[w]
                        qo, r_qo = qout[w]
                        P.op("dve", lambda e: e.scalar_tensor_tensor(out=qo[:], in0=qr_[:], scalar=vcol(V_GQN), in1=it["rs"][:],
                                                                     op0=ALU.mult, op1=ALU.mult),
                             reads=[r_qr, it["r_rs"], R_vec], writes=[r_qo])
                        P.dma("pool", "qst%d" % w, lambda e: e.dma_start(out=QN[h, :, j * TT:(j + 1) * TT], in_=qo[:]),
                              reads=[r_qo], writes=[R_QN])
                    yield

                def q_gen():
                    yield from q_window(0)
                    yield from q_window(4)
                    items = []
                    for hp in range(NH // 2):
                        aps, r_aps, _ = psr.next()
                        P.op("pe", lambda e: mm(e, aps[:], [(wQR[:, kc, hp * 128:(hp + 1) * 128], cqT[:, kc, :]) for kc in range(2)]),
                             reads=[r_wQR, r_cqT], writes=[r_aps])
                        it = dict(hp=hp, w=hp, sq=qsq[hp][0], r_sq=qsq[hp][1], rs=qrs[hp][0], r_rs=qrs[hp][1])
                        items.append(it)
                        a1, r_a1 = qt1[hp]
                        P.op("dve", lambda e: e.scalar_tensor_tensor(out=a1[:], in0=aps[:], scalar=vcol(V_GQR), in1=ctab[:], op0=ALU.mult, op1=ALU.mult),
                             reads=[r_aps, r_tab, R_vec], writes=[r_a1])
                        P.op("act", lambda e: e.activation(out=it["sq"][:], in_=aps[:], func=AF.Square), reads=[r_aps], writes=[it["r_sq"]])
                    yield
                    yield from norm_chain(items, psr, ones2, r_ones2, 1.0 / 64.0, eqb64, r_eqb64, None, do_square=False)
                    for it in items:
                        hp = it["hp"]
                        bps, r_bps, _ = psr.next()
                        P.op("pe", lambda e: mm(e, bps[:], [(wQS[:, kc, hp * 128:(hp + 1) * 128], cqT[:, kc, :]) for kc in range(2)]),
                             reads=[r_wQS, r_cqT], writes=[r_bps])
                        a2, r_a2 = qt2[it["w"]]
                        P.op("dve", lambda e: e.scalar_tensor_tensor(out=a2[:], in0=bps[:], scalar=dcol(DV_GQSS), in1=stab[:], op0=ALU.mult, op1=ALU.mult),
                             reads=[r_bps, r_tab, R_dvec], writes=[r_a2])
                    yield
                    for it in items:
                        a1, r_a1 = qt1[it["w"]]
                        a2, r_a2 = qt2[it["w"]]
                        P.op("pool", lambda e: e.tensor_tensor(out=a1[:], in0=a1[:], in1=a2[:], op=ALU.add), reads=[r_a1, r_a2], writes=[r_a1])
                    yield
                    for it in items:
                        w, hp = it["w"], it["hp"]
                        a1, r_a1 = qt1[w]
                        qo, r_qo = qout[w]
                        P.op("dve", lambda e: e.tensor_tensor(out=qo[:], in0=a1[:], in1=it["rs"][:], op=ALU.mult), reads=[r_a1, it["r_rs"]], writes=[r_qo])
                        P.dma("pool", "qst%d" % w, lambda e: e.dma_start(out=QR[hp, :, j * TT:(j + 1) * TT], in_=qo[:]), reads=[r_qo], writes=[R_QR])
                    yield
                    if j + 1 < nA2:
                        yield from rms_tile(x_own, (j + 1) * TT, xring, "xA2_", ssr[(j + 1) % 2][0], ssr[(j + 1) % 2][1], hbr, tpr,
                                 hTr[(j + 1) % 2][0], hTr[(j + 1) % 2][1])

                def y_gen():
                    yield from y_window(0)
                    yield from y_window(4)

                drain(y_gen())
                drain(q_gen())
        P.barrier()

        with contextlib.ExitStack() as st:
            def T(name, shape, dt):
                return st.enter_context(nc.sbuf_tensor(name, shape, dt))
            KrL = T("KrL", [128, S], BF16)
            KrH = T("KrH", [128, S], BF16)
            r_KrL, r_KrH = Res(), Res()
            MK = T("MK", [128, 8, TT], BF16)
            r_MK = Res()
            P.dma("sp", "krld", lambda e: e.dma_start(out=KrL[:], in_=KR), reads=[R_KR], writes=[r_KrL])
            P.dma("sp", "krld2", lambda e: e.dma_start(out=KrH[:], in_=KR), reads=[R_KR], writes=[r_KrH])
            P.op("pool", lambda e: e.memset(KrL[64:128, :], 0.0), reads=[r_KrL], writes=[r_KrL])
            P.op("pool", lambda e: e.memset(KrH[0:64, :], 0.0), reads=[r_KrH], writes=[r_KrH])
            P.dma("pool", "mkld", lambda e: e.dma_start(out=MK[:], in_=masks.rearrange("m p t -> p m t")), writes=[r_MK])
            Kt = [(T("Kt%d" % i, [128, S], BF16), Res()) for i in range(2)]
            Vt = [(T("Vt%d" % i, [128, S], BF16), Res()) for i in range(2)]
            Qn = [(T("Qn%d" % i, [128, S_OWN], BF16), Res()) for i in range(2)]
            Qr = [(T("Qr%d" % i, [128, S_OWN], BF16), Res()) for i in range(2)]
            spr = Ring(st, nc, "spsB_", 4, [128, TT], F32, psum=True)
            opr = Ring(st, nc, "opsB_", 2, [128, TT], F32, psum=True)
            lpr = Ring(st, nc, "lpsB_", 2, [128, TT], F32, psum=True)
            ptr = Ring(st, nc, "ptB_", 6, [128, TT], BF16)
            rlr = Ring(st, nc, "rlB_", 2, [128, TT], F32)
            yor = Ring(st, nc, "yoB_", 2, [128, TT], BF16)
            accr = Ring(st, nc, "accB_", 4, [128, TT], F32)
            onesf = T("onesf", [128, 128], F32)
            r_onesf = Res()
            P.op("pool", lambda e: e.memset(onesf[:], 1.0), writes=[r_onesf])

            def load_head(h):
                kt, r_kt = Kt[h % 2]
                vt, r_vt = Vt[h % 2]
                qn, r_qn = Qn[h % 2]
                P.dma("sp", "kld%d" % (h % 2), lambda e: e.dma_start(out=kt[:], in_=KT[h]), reads=[R_KT], writes=[r_kt])
                P.dma("sp", "vld%d" % (h % 2), lambda e: e.dma_start(out=vt[:], in_=VS[h]), reads=[R_VS], writes=[r_vt])
                P.dma("sp", "qld%d" % (h % 2), lambda e: e.dma_start(out=qn[:], in_=QN[h]), reads=[R_QN], writes=[r_qn])
                if h % 2 == 0:
                    qr, r_qr = Qr[(h // 2) % 2]
                    P.dma("sp", "qrld%d" % ((h // 2) % 2), lambda e: e.dma_start(out=qr[:], in_=QR[h // 2]), reads=[R_QR], writes=[r_qr])

            load_head(0)
            for h in range(lim("B", NH)):
                if h + 1 < NH:
                    load_head(h + 1)
                kt, r_kt = Kt[h % 2]
                vt, r_vt = Vt[h % 2]
                qn, r_qn = Qn[h % 2]
                qr, r_qr = Qr[(h // 2) % 2]
                Krh, r_Krh = (KrL, r_KrL) if h % 2 == 0 else (KrH, r_KrH)
                for j in range(NT_OWN):
                    nkb = 8 * j + 8
                    q0 = j * TT
                    ops_, r_ops, _ = opr.next()
                    lps_, r_lps, _ = lpr.next()
                    acc, r_acc, _ = accr.next()
                    acc2, r_acc2, _ = accr.next()
                    sbufs = {}

                    def issue_S(kb):
                        sp_, r_sp, _ = spr.next()
                        sbufs[kb] = (sp_, r_sp)
                        P.op("pe", lambda e, sp_=sp_, kb=kb: mm(e, sp_[:], [
                            (kt[:, kb * 128:(kb + 1) * 128], qn[:, q0:q0 + TT]),
                            (Krh[:, kb * 128:(kb + 1) * 128], qr[:, q0:q0 + TT])]),
                            reads=[r_kt, r_qn, r_Krh, r_qr], writes=[r_sp])

                    issue_S(0)
                    if nkb > 1:
                        issue_S(1)
                    for kb in range(nkb):
                        sp_, r_sp = sbufs.pop(kb)
                        pt, r_pt, _ = ptr.next()
                        P.op("act", lambda e, sp_=sp_, pt=pt: e.activation(out=pt[:], in_=sp_[:], func=AF.Exp, scale=ATT_SCALE),
                             reads=[r_sp], writes=[r_pt])
                        mi = kb - (nkb - 8)
                        if mi >= 0:
                            P.op("dve", lambda e, pt=pt, mi=mi: e.tensor_tensor(out=pt[:], in0=pt[:], in1=MK[:, mi, :], op=ALU.mult),
                                 reads=[r_pt, r_MK], writes=[r_pt])
                        if kb + 2 < nkb:
                            issue_S(kb + 2)

                        which = kb % 3
                        if which == 0:
                            def pv(e):
                                e.matmul(ops_[:], lhsT=vt[:, kb * 128:(kb + 1) * 128], rhs=pt[:], start=(kb == 0), stop=(kb == nkb - 1))
                                e.matmul(lps_[:], lhsT=ones[:], rhs=pt[:], start=(kb == 0), stop=False)
                            P.op("pe", pv, reads=[r_vt, r_pt, R_ones], writes=[r_ops, r_lps])
                        else:
                            eng = "dve" if which == 1 else "pool"
                            a_, r_a = (acc, r_acc) if which == 1 else (acc2, r_acc2)
                            if kb < 3:
                                P.op(eng, lambda e: e.tensor_copy(out=a_[:], in_=pt[:]), reads=[r_pt], writes=[r_a])
                            else:
                                P.op(eng, lambda e: e.tensor_tensor(out=a_[:], in0=a_[:], in1=pt[:], op=ALU.add), reads=[r_pt, r_a], writes=[r_a])
                            P.op("pe", lambda e: e.matmul(ops_[:], lhsT=vt[:, kb * 128:(kb + 1) * 128], rhs=pt[:], start=(kb == 0), stop=(kb == nkb - 1)),
                                 reads=[r_vt, r_pt], writes=[r_ops])

                    def lfin(e):
                        e.matmul(lps_[:], lhsT=onesf[:], rhs=acc[:], start=False, stop=False)
                        e.matmul(lps_[:], lhsT=onesf[:], rhs=acc2[:], start=False, stop=True)
                    P.op("pe", lfin, reads=[r_onesf, r_acc, r_acc2], writes=[r_lps])
                    rl, r_rl, _ = rlr.next()
                    P.op("dve", lambda e, rl=rl, lps_=lps_: e.reciprocal(out=rl[:], in_=lps_[:]), reads=[r_lps], writes=[r_rl])
                    yo, r_yo, yi = yor.next()
                    P.op("dve", lambda e, yo=yo, ops_=ops_, rl=rl: e.tensor_tensor(out=yo[:], in0=ops_[:], in1=rl[:], op=ALU.mult),
                         reads=[r_ops, r_rl], writes=[r_yo])
                    P.dma("pool", "yast%d" % yi, lambda e, yo=yo, h=h, q0=q0: e.dma_start(out=YA[h, :, q0:q0 + TT], in_=yo[:]),
                          reads=[r_yo], writes=[R_YA])
        P.barrier()

        with contextlib.ExitStack() as st:
            def T(name, shape, dt):
                return st.enter_context(nc.sbuf_tensor(name, shape, dt))
            stage = Ring(st, nc, "stgC1_", 2, [128, 1024], F32)
            wC, r_wC = load_scaled_weight(st, "wC", winC, 8, 2048, V_GMIX, stage, "stg")
            wPA, r_wPA = load_cast_weight(st, "wPA", wpa, 8, D, "wcast")
            wPL, r_wPL = load_cast_weight(st, "wPL", wpl, 8, D, "wcast")
            wO, r_wO = load_cast_weight(st, "wO", wo, 8, D, "wcast")
            xring = Ring(st, nc, "xC1_", 8, [128, D], F32)
            ssr = [(T("ssC1_%d" % i, [128, 4], F32), [Res() for _ in range(4)]) for i in range(2)]
            hbr = Ring(st, nc, "hbC1_", 4, [128, D], BF16)
            hTr = [(T("hTC1_%d" % i, [128, 8, TT], BF16), [Res() for _ in range(4)]) for i in range(2)]
            tpr = Ring(st, nc, "tpC1_", 2, [128, D], BF16, psum=True)
            psr = Ring(st, nc, "psC1_", 6, [128, TT], F32, psum=True)
            yaT = [(T("yaT%d" % i, [128, 8, TT], BF16), Res()) for i in range(2)]
            ybT = [(T("ybT%d" % i, [128, 8, TT], BF16), Res()) for i in range(2)]
            sga = Ring(st, nc, "sga_", 2, [128, TT], F32)
            sgb = Ring(st, nc, "sgb_", 2, [128, TT], F32)
            mT = [(T("mT%d" % i, [128, 8, TT], BF16), [Res() for _ in range(8)]) for i in range(2)]

            nC1 = lim("C1", NT_OWN)
            xs_next = []
            if nC1 > 0:
                drain(rms_tile(x_own, 0, xring, "xC1_", ssr[0][0], ssr[0][1], hbr, tpr, hTr[0][0], hTr[0][1], xs_out=xs_next))
            for j in range(nC1):
                hT, r_hT = hTr[j % 2]
                ya, r_ya = yaT[j % 2]
                yb, r_yb = ybT[j % 2]
                mt, r_mt = mT[j % 2]
                P.dma("sp", "yald%d" % (j % 2), lambda e, ya=ya, j=j: e.dma_start(
                    out=ya[:], in_=YA[:, :, j * TT:(j + 1) * TT].rearrange("h p t -> p h t")), reads=[R_YA], writes=[r_ya])
                P.dma("sp", "ybld%d" % (j % 2), lambda e, yb=yb, j=j: e.dma_start(
                    out=yb[:], in_=YB[:, :, j * TT:(j + 1) * TT].rearrange("c p t -> p c t")), reads=[R_YB], writes=[r_yb])
                xs = xs_next
                for m in range(8):
                    if m == 4 and j + 1 < nC1:
                        xs_next = []
                        drain(rms_tile(x_own, (j + 1) * TT, xring, "xC1_", ssr[(j + 1) % 2][0], ssr[(j + 1) % 2][1], hbr, tpr,
                                       hTr[(j + 1) % 2][0], hTr[(j + 1) % 2][1], xs_out=xs_next))
                    gaps, r_gaps, _ = psr.next()
                    gbps, r_gbps, _ = psr.next()
                    P.op("pe", lambda e, gaps=gaps, m=m: mm(e, gaps[:], [(wC[:, kc, m * 128:(m + 1) * 128], hT[:, kc, :]) for kc in range(8)]),
                         reads=[r_wC] + r_hT, writes=[r_gaps])
                    P.op("pe", lambda e, gbps=gbps, m=m: mm(e, gbps[:], [(wC[:, kc, 1024 + m * 128:1024 + (m + 1) * 128], hT[:, kc, :]) for kc in range(8)]),
                         reads=[r_wC] + r_hT, writes=[r_gbps])
                    sa, r_sa, _ = sga.next()
                    sb, r_sb, _ = sgb.next()
                    P.op("act", lambda e, gaps=gaps, sa=sa: e.activation(out=sa[:], in_=gaps[:], func=AF.Sigmoid), reads=[r_gaps], writes=[r_sa])
                    P.op("act", lambda e, gbps=gbps, sb=sb: e.activation(out=sb[:], in_=gbps[:], func=AF.Sigmoid), reads=[r_gbps], writes=[r_sb])
                    paps, r_paps, _ = psr.next()
                    plps, r_plps, _ = psr.next()
                    P.op("pe", lambda e, paps=paps, m=m: mm(e, paps[:], [(wPA[:, kc, m * 128:(m + 1) * 128], ya[:, kc, :]) for kc in range(8)]),
                         reads=[r_wPA, r_ya], writes=[r_paps])
                    P.op("pe", lambda e, plps=plps, m=m: mm(e, plps[:], [(wPL[:, kc, m * 128:(m + 1) * 128], yb[:, kc, :]) for kc in range(8)]),
                         reads=[r_wPL, r_yb], writes=[r_plps])
                    P.op("dve", lambda e, sa=sa, paps=paps: e.tensor_tensor(out=sa[:], in0=paps[:], in1=sa[:], op=ALU.mult),
                         reads=[r_paps, r_sa], writes=[r_sa])
                    P.op("dve", lambda e, sb=sb, plps=plps: e.tensor_tensor(out=sb[:], in0=plps[:], in1=sb[:], op=ALU.mult),
                         reads=[r_plps, r_sb], writes=[r_sb])
                    P.op("dve", lambda e, sa=sa, sb=sb, m=m: e.tensor_tensor(out=mt[:, m, :], in0=sa[:], in1=sb[:], op=ALU.add),
                         reads=[r_sa, r_sb], writes=[r_mt[m]])
                for s in range(4):
                    xt, rx, si = xs[s]
                    pss = [psr.next() for _ in range(2)]

                    def f(e):
                        for kc in range(8):
                            for half in range(2):
                                e.matmul(pss[half][0][:], lhsT=mt[:, kc, s * 128:(s + 1) * 128], rhs=wO[:, kc, half * 512:(half + 1) * 512],
                                         start=(kc == 0), stop=(kc == 7))
                    P.op("pe", f, reads=r_mt + [r_wO], writes=[pss[0][1], pss[1][1]])
                    for half in range(2):
                        P.op("dve", lambda e: e.tensor_tensor(out=xt[:, half * 512:(half + 1) * 512], in0=pss[half][0][:],
                                                              in1=xt[:, half * 512:(half + 1) * 512], op=ALU.add),
                             reads=[pss[half][1], rx], writes=[rx])
                    r0 = j * TT + s * 128
                    P.dma("pool", "x1st%d" % si, lambda e, xt=xt, r0=r0: e.dma_start(out=X1[r0:r0 + 128, :], in_=xt[:]),
                          reads=[rx], writes=[R_X1])
        P.barrier()

        with contextlib.ExitStack() as st:
            def T(name, shape, dt):
                return st.enter_context(nc.sbuf_tensor(name, shape, dt))
            wG = T("wG", [128, 8, FF], BF16)
            wU = T("wU", [128, 8, FF], BF16)
            wD, r_wD = load_cast_weight(st, "wD", wfd, NF, D, "wcast")
            with contextlib.ExitStack() as st2:
                stage = Ring(st2, nc, "stgC2_", 2, [128, 1024], F32)
                _, r_wG = load_scaled_weight(st, "wG", wfg, 8, FF, V_GFFN, stage, "stg", w=wG)
                _, r_wU = load_scaled_weight(st, "wU", wfu, 8, FF, V_GFFN, stage, "stg", w=wU)
            P.barrier()
            xring = Ring(st, nc, "xC2_", 4, [128, D], F32)
            xrr = Ring(st, nc, "xrC2_", 2, [128, D], F32)
            ssr = [(T("ssC2_%d" % i, [128, 4], F32), [Res() for _ in range(4)]) for i in range(2)]
            hbr = Ring(st, nc, "hbC2_", 4, [128, D], BF16)
            hTr = [(T("hTC2_%d" % i, [128, 8, TT], BF16), [Res() for _ in range(4)]) for i in range(1)]
            tpr = Ring(st, nc, "tpC2_", 2, [128, D], BF16, psum=True)
            psr = Ring(st, nc, "psC2_", 6, [128, TT], F32, psum=True)
            aT = T("aT", [128, NF, TT], BF16)
            r_aT = [Res() for _ in range(NF)]
            slr = Ring(st, nc, "slr_", 2, [128, TT], F32)

            nC2 = lim("C2", NT_OWN)
            hT, r_hT = hTr[0]
            if nC2 > 0:
                drain(rms_tile(X1, 0, xring, "xC2_", ssr[0][0], ssr[0][1], hbr, tpr, hT, r_hT, rd=[R_X1]))
            for j in range(nC2):
                for m in range(NF):
                    gps, r_gps, _ = psr.next()
                    ups, r_ups, _ = psr.next()
                    P.op("pe", lambda e: mm(e, gps[:], [(wG[:, kc, m * 128:(m + 1) * 128], hT[:, kc, :]) for kc in range(8)]),
                         reads=[r_wG] + r_hT, writes=[r_gps])
                    P.op("pe", lambda e: mm(e, ups[:], [(wU[:, kc, m * 128:(m + 1) * 128], hT[:, kc, :]) for kc in range(8)]),
                         reads=[r_wU] + r_hT, writes=[r_ups])
                    sl, r_sl, _ = slr.next()
                    P.op("act", lambda e: e.activation(out=sl[:], in_=gps[:], func=AF.Silu), reads=[r_gps], writes=[r_sl])
                    P.op("dve", lambda e: e.tensor_tensor(out=aT[:, m, :], in0=ups[:], in1=sl[:], op=ALU.mult),
                         reads=[r_ups, r_sl], writes=[r_aT[m]])
                if j + 1 < nC2:
                    drain(rms_tile(X1, (j + 1) * TT, xring, "xC2_", ssr[(j + 1) % 2][0], ssr[(j + 1) % 2][1], hbr, tpr, hT, r_hT, rd=[R_X1]))
                for s in range(4):
                    xt, rx, si = xrr.next()
                    r0 = j * TT + s * 128
                    P.dma("sp", "xrC2_%d" % si, lambda e: e.dma_start(out=xt[:], in_=X1[r0:r0 + 128, :]), reads=[R_X1], writes=[rx])
                    pss = [psr.next() for _ in range(2)]

                    def f(e):
                        for kc in range(NF):
                            for half in range(2):
                                e.matmul(pss[half][0][:], lhsT=aT[:, kc, s * 128:(s + 1) * 128], rhs=wD[:, kc, half * 512:(half + 1) * 512],
                                         start=(kc == 0), stop=(kc == NF - 1))
                    P.op("pe", f, reads=r_aT + [r_wD], writes=[pss[0][1], pss[1][1]])
                    for half in range(2):
                        P.op("dve", lambda e: e.tensor_tensor(out=xt[:, half * 512:(half + 1) * 512], in0=pss[half][0][:],
                                                              in1=xt[:, half * 512:(half + 1) * 512], op=ALU.add),
                             reads=[pss[half][1], rx], writes=[rx])
                    od = P.dma("pool", "ost%d" % si, lambda e: e.dma_start(out=out[r0:r0 + 128, :], in_=xt[:]), reads=[rx])
                    out_dmas.append(od)
        P.finish(out_dmas)
        P.emit()
    return nc


_NC_CACHE = {}
_PHASES = ("A1", "A2", "B", "C1", "C2")
_DBG = ""


def _prep_inputs(x, positions, norm_mix_g, w_in, q_lat_g, w_q_up, kv_lat_g, w_kv_up, q_head_g, k_head_g,
                 conv_w, conv_b, lru_wa, lru_ba, lru_wx, lru_bx, lru_lambda, w_proj_attn, w_proj_lru,
                 w_out, norm_ffn_g, w_ffn_gate, w_ffn_up, w_ffn_down):
    f32 = np.float32
    A = lambda a: np.ascontiguousarray(np.asarray(a))
    x = A(x)
    positions = A(positions)
    w_in = A(w_in)[0]
    perm = np.concatenate([np.arange(32, 64), np.arange(0, 32)])
    kr = w_in[:, 512:576]
    sw = kr[:, perm]
    winA1 = A(np.concatenate([w_in[:, 256:512], kr, kr, sw, sw, w_in[:, 576:1600]], axis=1))
    winA2 = A(np.concatenate([w_in[:, 0:256], w_in[:, 1600:2624]], axis=1))
    winC = A(w_in[:, 2624:4672])
    wkv = A(w_kv_up)[0].reshape(256, NH, 2, 128)
    wkvK = A(wkv[:, :, 0, :].reshape(256, 1024))
    wkvV = A(wkv[:, :, 1, :].reshape(256, 1024))
    wq = A(w_q_up)[0].reshape(256, NH, 192)
    wqN = A(wq[:, :, :128].reshape(256, 1024))
    wqR = A(wq[:, :, 128:].reshape(256, 512))
    wqS = A(wq[:, :, 128:][:, :, perm].reshape(256, 512))
    wa = A(lru_wa)[0]
    wx = A(lru_wx)[0]
    lruA = np.zeros((8, 128, 128), f32)
    lruX = np.zeros((8, 128, 128), f32)
    for c in range(8):
        for b in range(2):
            lruA[c, b * 64:(b + 1) * 64, b * 64:(b + 1) * 64] = wa[2 * c + b]
            lruX[c, b * 64:(b + 1) * 64, b * 64:(b + 1) * 64] = wx[2 * c + b]

    def pc(v, n):
        return A(np.asarray(v, f32).reshape(n, 128).T)

    p = np.arange(128)
    qg = A(q_head_g)[0]
    kg = A(k_head_g)[0]
    vecs = np.zeros((128, NV), f32)
    vecs[:, V_GMIX:V_GMIX + 8] = pc(A(norm_mix_g)[0], 8)
    vecs[:, V_GFFN:V_GFFN + 8] = pc(A(norm_ffn_g)[0], 8)
    vecs[:, V_GQL:V_GQL + 2] = pc(A(q_lat_g)[0], 2)
    vecs[:, V_GKVL:V_GKVL + 2] = pc(A(kv_lat_g)[0], 2)
    vecs[:, V_GQN] = qg[:128]
    vecs[:, V_GQR] = qg[128 + (p % 64)]
    vecs[:, V_GQS] = qg[128 + ((p % 64) + 32) % 64]
    vecs[:, V_GKN] = kg[:128]
    vecs[:, V_GKR] = kg[128 + (p % 64)]
    vecs[:, V_GKS] = kg[128 + ((p % 64) + 32) % 64]
    vecs[:, V_SGN] = np.where((p % 64) < 32, -1.0, 1.0).astype(f32)
    half = 32
    inv_freq = (np.float32(10000.0) ** (-(np.arange(half, dtype=f32) / np.float32(half)))).astype(f32)
    vecs[:, V_INVF] = inv_freq[p % 32]
    cw = A(conv_w)[0]
    for c in range(8):
        for tap in range(4):
            vecs[:, V_CONVW + c * 4 + tap] = cw[tap, c * 128:(c + 1) * 128]
    vecs[:, V_CONVB:V_CONVB + 8] = pc(A(conv_b)[0], 8)
    vecs[:, V_BA:V_BA + 8] = pc(A(lru_ba)[0], 8)
    vecs[:, V_BX:V_BX + 8] = pc(A(lru_bx)[0], 8)
    vecs[:, V_LAM:V_LAM + 8] = pc(A(lru_lambda)[0], 8)

    kk = np.arange(128)[:, None]
    qq = np.arange(TT)[None, :]
    Md = [((128 * d + kk) <= qq).astype(f32) for d in range(4)]
    one = np.ones((128, TT), f32)
    zero = np.zeros((128, TT), f32)
    mask_par = {1: np.stack([one, one, one, one] + Md), 0: np.stack(Md + [zero, zero, zero, zero])}

    common = dict(winA1=winA1, winA2=winA2, winC=winC, wkvK=wkvK, wkvV=wkvV, wqN=wqN, wqR=wqR, wqS=wqS,
                  lruA=lruA, lruX=lruX, wpa=A(w_proj_attn)[0], wpl=A(w_proj_lru)[0], wo=A(w_out)[0],
                  wfg=A(w_ffn_gate)[0], wfu=A(w_ffn_up)[0], wfd=A(w_ffn_down)[0])
    in_maps = []
    for c in range(8):
        b, par = c // 2, c % 2
        xb = x[b]
        v = vecs.copy()
        v[:, V_M0] = 1.0 if par == 0 else 0.0
        v[:, V_M1] = 1.0 if par == 1 else 0.0
        m = dict(common)
        m["x_all"] = xb
        m["x_own"] = A(xb.reshape(NT_ALL, TT, D)[par::2].reshape(S_OWN, D))
        m["pos_all"] = A(positions[b].reshape(1, S).astype(np.int32))
        m["pos_own"] = A(positions[b].reshape(NT_ALL, TT)[par::2].reshape(1, S_OWN).astype(np.int32))
        m["vecs"] = v
        m["masks"] = mask_par[par]
        in_maps.append(m)
    return in_maps


def kernel(**inputs):
    in_maps = _prep_inputs(**inputs)
    if "nc" not in _NC_CACHE:
        _NC_CACHE["nc"] = build_program()
    nc = _NC_CACHE["nc"]
    res = run_bass_kernel_spmd(nc, in_maps, core_ids=list(range(8)))
    outp = np.empty((4, S, D), np.float32)
    for c in range(8):
        b, par = c // 2, c % 2
        o = np.asarray(res.results[c]["out"]).reshape(NT_OWN, TT, D)
        outp[b].reshape(NT_ALL, TT, D)[par::2] = o
    return outp
```
